# Optimizing a Trainium2 kernel written in Bass

```python
import jax, jax.numpy as jnp
from jax import lax
import numpy as np

D_MODEL = 1024
BATCH = 32
SEQ = 2048
DEPTH = 1

CHUNK = 64
N_META = 16
EPS = 1e-6
POOL_WIDTH = 512
POOL_GROUPS = 4
POOL_GROUP_DIM = POOL_WIDTH // POOL_GROUPS
POOL_WINDOWS = (2, 4, 8, 16)
GLA_HEADS = 4
GLA_DK = 64
GLA_DV = 128
GLA_KEY = GLA_HEADS * GLA_DK
GLA_VAL = GLA_HEADS * GLA_DV
GLA_GATE_RANK = 16
GLA_TAU = 16.0
D_FF = 4 * D_MODEL
IN_SIZES = (POOL_WIDTH, GLA_KEY, GLA_KEY, GLA_VAL, GLA_VAL, GLA_GATE_RANK, D_MODEL, D_MODEL)
N_IN = sum(IN_SIZES)
IN_SPLITS = [int(s) for s in np.cumsum(IN_SIZES)[:-1]]

kernel_name = "hybrid_pool_gla_gated_block"


def rmsnorm(x, g):
    xf = x.astype(jnp.float32)
    y = xf * lax.rsqrt(jnp.mean(xf * xf, axis=-1, keepdims=True) + EPS)
    return (y * g.astype(jnp.float32)).astype(x.dtype)


def multiscale_pool(a):
    B, L, _ = a.shape
    af = a.astype(jnp.float32)
    cs = jnp.concatenate([jnp.zeros((B, 1, POOL_WIDTH), jnp.float32), jnp.cumsum(af, axis=1)], axis=1)
    t = jnp.arange(L)
    outs = []
    for g, w in enumerate(POOL_WINDOWS):
        lo_c, hi_c = g * POOL_GROUP_DIM, (g + 1) * POOL_GROUP_DIM
        hi = cs[:, 1:, lo_c:hi_c]
        lo = jnp.concatenate([jnp.zeros((B, w - 1, POOL_GROUP_DIM), jnp.float32), cs[:, :L - w + 1, lo_c:hi_c]], axis=1)
        cnt = jnp.minimum(t + 1, w).astype(jnp.float32)[None, :, None]
        outs.append((hi - lo) / cnt)
    return (jnp.concatenate(outs, axis=-1) - af).astype(a.dtype)


def gla_chunk_step(S, inp):
    qc, kc, vc, ac = inp
    b = jnp.cumsum(ac, axis=2)
    decay = jnp.exp(-jnp.abs(b[:, :, :, None, :] - b[:, :, None, :, :]))
    scores = jnp.einsum('bhtk,bhsk,bhtsk->bhts', qc, kc, decay)
    o = jnp.einsum('bhts,bhsv->bhtv', scores, vc) + jnp.einsum('bhtk,bhkv->bhtv', qc * jnp.exp(b), S)
    b_end = b[:, :, -1:, :]
    S_new = jnp.exp(b_end[:, :, 0, :])[..., None] * S + jnp.einsum('bhsk,bhsv->bhkv', kc * jnp.exp(b_end - b), vc)
    return S_new, o


def gla_chunked(q, k, v, log_a):
    B = q.shape[0]
    pad = CHUNK - N_META

    def prep(t):
        t = jnp.pad(t.astype(jnp.float32), ((0, 0), (pad, 0), (0, 0), (0, 0)))
        n = t.shape[1] // CHUNK
        return t.reshape(B, n, CHUNK, GLA_HEADS, t.shape[-1]).transpose(1, 0, 3, 2, 4)

    qc, kc, vc, ac = prep(q), prep(k), prep(v), prep(log_a)
    S0 = jnp.zeros((B, GLA_HEADS, GLA_DK, GLA_DV), jnp.float32)
    _, o = lax.scan(gla_chunk_step, S0, (qc, kc, vc, ac))
    o = o.transpose(1, 0, 3, 2, 4).reshape(B, -1, GLA_HEADS, GLA_DV)[:, pad:]
    return o.astype(v.dtype)


def hybrid_mixer(u, w_in, w_pool_mix, pool_scale, w_pool_out, w_gate_up, b_gate, gla_norm, w_gla_out, w_out):
    B, L, _ = u.shape
    z = jnp.einsum('bld,dn->bln', u, w_in)
    a, q, k, v, r, lr, ga, gb = jnp.split(z, IN_SPLITS, axis=-1)
    p = multiscale_pool(a).reshape(B, L, POOL_GROUPS, POOL_GROUP_DIM)
    p = jnp.einsum('blgc,gce->blge', p, w_pool_mix).reshape(B, L, POOL_WIDTH) * pool_scale
    y_a = jnp.einsum('blc,cd->bld', p, w_pool_out)
    log_a = jax.nn.log_sigmoid((lr @ w_gate_up + b_gate).astype(jnp.float32)) / GLA_TAU
    q = q.reshape(B, L, GLA_HEADS, GLA_DK) * (GLA_DK ** -0.5)
    k = k.reshape(B, L, GLA_HEADS, GLA_DK)
    v = v.reshape(B, L, GLA_HEADS, GLA_DV)
    log_a = log_a.reshape(B, L, GLA_HEADS, GLA_DK)
    o = gla_chunked(q, k, v, log_a)
    o = rmsnorm(o, gla_norm.reshape(GLA_HEADS, GLA_DV)) * jax.nn.silu(r).reshape(B, L, GLA_HEADS, GLA_DV)
    y_b = jnp.einsum('blc,cd->bld', o.reshape(B, L, GLA_VAL), w_gla_out)
    m = jax.nn.sigmoid(ga) * y_a + jax.nn.sigmoid(gb) * y_b
    return jnp.einsum('bld,de->ble', m, w_out)


def squared_relu_mlp(u, w_ff1, w_ff2):
    hdn = jnp.square(jax.nn.relu(jnp.einsum('bld,df->blf', u, w_ff1)))
    return jnp.einsum('blf,fd->bld', hdn, w_ff2)


def setup_inputs(seed: int = 0) -> dict:
    key = jax.random.key(seed)
    ks = jax.random.split(key, 20)
    f32 = jnp.float32
    nrm = lambda k, shape, s: jax.random.normal(k, shape, f32) * s
    return {
        "x": jax.random.normal(ks[0], (BATCH, SEQ, D_MODEL), f32),
        "meta_tokens": nrm(ks[1], (N_META, D_MODEL), 1.0),
        "norm_mix": 1.0 + nrm(ks[2], (DEPTH, D_MODEL), 0.05),
        "w_in": nrm(ks[3], (DEPTH, D_MODEL, N_IN), D_MODEL ** -0.5),
        "w_pool_mix": nrm(ks[4], (DEPTH, POOL_GROUPS, POOL_GROUP_DIM, POOL_GROUP_DIM), POOL_GROUP_DIM ** -0.5),
        "pool_scale": 1.0 + nrm(ks[5], (DEPTH, POOL_WIDTH), 0.1),
        "w_pool_out": nrm(ks[6], (DEPTH, POOL_WIDTH, D_MODEL), POOL_WIDTH ** -0.5),
        "w_gate_up": nrm(ks[7], (DEPTH, GLA_GATE_RANK, GLA_KEY), GLA_GATE_RANK ** -0.5),
        "b_gate": nrm(ks[8], (DEPTH, GLA_KEY), 0.1),
        "gla_norm": 1.0 + nrm(ks[9], (DEPTH, GLA_VAL), 0.05),
        "w_gla_out": nrm(ks[10], (DEPTH, GLA_VAL, D_MODEL), GLA_VAL ** -0.5),
        "w_out": nrm(ks[11], (DEPTH, D_MODEL, D_MODEL), D_MODEL ** -0.5),
        "norm_ffn": 1.0 + nrm(ks[12], (DEPTH, D_MODEL), 0.05),
        "w_ff1": nrm(ks[13], (DEPTH, D_MODEL, D_FF), D_MODEL ** -0.5),
        "w_ff2": nrm(ks[14], (DEPTH, D_FF, D_MODEL), D_FF ** -0.5),
        "norm_final": 1.0 + nrm(ks[15], (D_MODEL,), 0.05),
    }


def reference(x, meta_tokens, norm_mix, w_in, w_pool_mix, pool_scale, w_pool_out, w_gate_up, b_gate,
              gla_norm, w_gla_out, w_out, norm_ffn, w_ff1, w_ff2, norm_final):
    B = x.shape[0]
    meta = jnp.broadcast_to(meta_tokens[None].astype(x.dtype), (B, N_META, D_MODEL))
    h = jnp.concatenate([meta, x], axis=1)
    for l in range(DEPTH):
        h = h + hybrid_mixer(rmsnorm(h, norm_mix[l]), w_in[l], w_pool_mix[l], pool_scale[l], w_pool_out[l],
                             w_gate_up[l], b_gate[l], gla_norm[l], w_gla_out[l], w_out[l])
        h = h + squared_relu_mlp(rmsnorm(h, norm_ffn[l]), w_ff1[l], w_ff2[l])
    return rmsnorm(h, norm_final)[:, N_META:]
```

```python
import numpy as np
from contextlib import ExitStack
import concourse.bass as bass
import concourse.mybir as mybir
from concourse.bass_utils import run_bass_kernel_spmd

F32 = mybir.dt.float32
BF16 = mybir.dt.bfloat16
AF = mybir.ActivationFunctionType
ALU = mybir.AluOpType

D = 1024
T = 512
NSLOTS = 5
EPS = 1e-6
N_CORES = 8

C_MCAT, C_CM, C_ONES, C_G3 = 0, 512, 1024, 1152
C_G1P, C_G2P, C_PSC, C_GNP, C_BG, C_EPS = 2176, 2184, 2192, 2196, 2200, 2202
NC32 = 2208
B_ID, B_BP, B_BC = 0, 128, 640
NCB = 1152

PE, ACT, DVE, POOL, SP = "pe", "act", "dve", "pool", "sp"
COMPUTE = (PE, ACT, DVE, POOL)


class Op:
    pass


class Prog:
    def __init__(self):
        self.ops = []
        self.by_eng = {e: [] for e in (PE, ACT, DVE, POOL, SP)}
        self.last_w = {}
        self.readers = {}
        self.dma_count = {}

    def add(self, eng, fn, reads=(), writes=(), dma_sem=None):
        op = Op()
        op.eng, op.fn, op.idx = eng, fn, len(self.ops)
        op.pos = len(self.by_eng[eng])
        op.dma_sem = dma_sem
        op.target = False
        op.rank = None
        if dma_sem is not None:
            self.dma_count[dma_sem] = self.dma_count.get(dma_sem, 0) + 16
            op.dma_val = self.dma_count[dma_sem]
        deps = {}
        for k in reads:
            w = self.last_w.get(k)
            if w is not None:
                deps[w] = "raw"
        for k in writes:
            w = self.last_w.get(k)
            if w is not None and w not in deps:
                deps[w] = "waw"
            for r in self.readers.get(k, ()):
                if r not in deps:
                    deps[r] = "war"
        deps.pop(op.idx, None)
        op.deps = deps
        for k in reads:
            self.readers.setdefault(k, []).append(op.idx)
        for k in writes:
            self.last_w[k] = op.idx
            self.readers[k] = []
        self.ops.append(op)
        self.by_eng[eng].append(op)
        return op

    def alias(self, old_keys, new_keys):
        acc = []
        for k in old_keys:
            w = self.last_w.get(k)
            if w is not None:
                acc.append(w)
            acc.extend(self.readers.get(k, ()))
        for k in new_keys:
            self.last_w.pop(k, None)
            self.readers[k] = list(acc)


def build(n_seq=4, n_tiles=4, debug=False):
    nc = bass.Bass("TRN2", target_bir_lowering=False)
    P = Prog()

    def din(name, shape):
        return nc.dram_tensor(name, list(shape), F32, kind="ExternalInput").ap()

    x = din("x", [n_seq, n_tiles * T, D])
    xmeta = din("xmeta", [T, D])
    cst32_d = din("cst32", [128, NC32])
    cstb_d = din("cstb", [128, NCB])
    wall_d = din("wall", [28, 128, 4096])
    w_lr_d = din("w_lr_l", [128, 8, 16])
    w_pm_d = din("w_pm_l", [128, 4, 128])
    w_gu_d = din("w_gu", [16, 256])
    out = nc.dram_tensor("out", [n_seq, n_tiles * T, D], F32, kind="ExternalOutput").ap()
    NPIECE = 28
    wsc = nc.dram_tensor("wsc", [NPIECE, 128, 4096], BF16, kind="Internal").ap()

    off = [16512]

    def sb(name, shape, dtype, at=None):
        esz = 4 if dtype == F32 else 2
        n = 1
        for s_ in shape[1:]:
            n *= s_
        nbytes = n * esz
        if at is None:
            at_ = off[0]
            off[0] += (nbytes + 63) // 64 * 64
        else:
            at_ = at
        t = nc.alloc_sbuf_tensor_at(name, list(shape), dtype, offset=at_)
        return t, at_, nbytes

    CST, _, _ = sb("cst", [128, NC32], F32)
    IDENT, _, _ = sb("ident", [128, 128], BF16)
    ONESB = CST[:, C_ONES:C_ONES + 64].bitcast(BF16)
    BANDP, _, _ = sb("bandp", [128, 4, 128], BF16)
    BANDC, _, _ = sb("bandc", [128, 4, 128], BF16)
    WLR, _, _ = sb("wlr", [128, 8, 16], BF16)
    WPM, _, _ = sb("wpm", [128, 4, 128], BF16)
    WGU, _, _ = sb("wgu", [16, 256], BF16)
    NEGB, _, _ = sb("negb", [128, 2], F32)
    H0, h_at, _ = sb("h0", [128, 4, D], F32)
    H1, _, _ = sb("h1", [128, 4, D], F32)
    HB = [H0, H1]
    HN, hn_at, _ = sb("hn", [128, 4, D], BF16)
    MT, _, _ = sb("mt", [128, 8, T], BF16, at=hn_at)
    UT, _, _ = sb("ut", [128, 8, T], BF16)
    ATOK, _, _ = sb("atok", [128, 5, 512], BF16)
    AMETA, _, _ = sb("ameta", [128, 512], BF16)
    VTOK, _, _ = sb("vtok", [128, 4, 512], BF16)
    SR, _, _ = sb("sr", [128, 4, T], F32)
    SS, ss_at, _ = sb("ss", [128, 4], F32)
    WARM = sb("warm", [128, 1], F32, at=ss_at + 32)[0]
    RS, _, _ = sb("rs", [128, 4], F32)
    HDN, r1, _ = sb("hdn", [128, 32, T], BF16)
    E1L, _, _ = sb("e1l", [128, 2, T], F32, at=r1)
    CL, _, _ = sb("cl", [128, 2, T], F32, at=r1 + 4096)
    EP, _, _ = sb("ep", [128, 2, T], F32, at=r1 + 8192)
    EM, _, _ = sb("em", [128, 2, T], F32, at=r1 + 12288)
    QK32, _, _ = sb("qk32", [128, 4, T], F32, at=r1 + 16384)
    KM32, _, _ = sb("km32", [128, 2, T], F32, at=r1 + 24576)
    KT, _, _ = sb("kt", [128, 2, T], BF16, at=r1 + 28672)
    LRT, _, _ = sb("lrt", [16, T], BF16, at=r1 + 30720)
    STG = [sb(f"stg{i}", [128, 4096], F32, at=r1 + 16384 * i)[0] for i in range(2)]
    STG.append(sb("stg2", [128, 4096], F32, at=h_at)[0])
    STG.append(sb("stg3", [128, 4096], F32, at=hn_at)[0])
    QP, _, _ = sb("qp", [128, 2, T], BF16)
    QM, _, _ = sb("qm", [128, 2, T], BF16)
    KP, _, _ = sb("kp", [128, 2, T], BF16)
    KM, _, _ = sb("km", [128, 2, T], BF16)
    KTE, _, _ = sb("kte", [128, 4, 256], BF16)
    KTO, _, _ = sb("kto", [128, 4, 256], BF16)
    DF, _, _ = sb("df", [128, 2, 8], F32)
    SB2 = [sb(f"s{i}", [128, 2, 128], F32)[0] for i in range(3)]
    SMETA, _, _ = sb("smeta", [128, 2, 128], F32)
    SX, _, _ = sb("sx", [128, 9, 2, 2, 128], BF16)
    OG, _, _ = sb("og", [128, 4, T], BF16)
    PT, _, _ = sb("pt", [128, 4, T], BF16)
    P2T, _, _ = sb("p2t", [128, 4, T], BF16)
    R2T, r2, _ = sb("r2t", [128, 4608], F32)
    T32B = sb("t32b", [128, 4, 2, 512], BF16, at=r2)[0]
    OSQ = [sb(f"osq_{i}", [128, 512], BF16, at=r2 + 8192 + 2048 * i)[0] for i in range(2)]
    RO = [sb(f"ro_{i}", [128, 512], F32, at=r2 + 12288 + 2048 * i)[0] for i in range(2)]
    T1 = sb("t1", [128, 512], F32, at=r2 + 16384)[0]
    SGA = [sb(f"sga_{i}", [128, 512], F32, at=r2 + 2048 * i)[0] for i in range(2)]
    SGB = [sb(f"sgb_{i}", [128, 512], F32, at=r2 + 4096 + 2048 * i)[0] for i in range(2)]
    TM1 = [sb(f"tm1_{i}", [128, 512], F32, at=r2 + 8192 + 2048 * i)[0] for i in range(2)]
    TM2 = [sb(f"tm2_{i}", [128, 512], F32, at=r2 + 12288 + 2048 * i)[0] for i in range(2)]
    R32 = [sb(f"r32_{i}", [128, 512], F32, at=r2 + 2048 * i)[0] for i in range(4)]
    OUTB = sb("outb", [128, 2, D], F32, at=r2 + 8192)[0]
    JNK = sb("jnk", [128, 2, D], BF16, at=r2)[0]
    RING = [sb(f"ring{i}", [128, 4096], BF16)[0] for i in range(NSLOTS)]
    assert off[0] <= 229376, off[0]

    KEYS_GLA = ["e1l0", "e1l1", "cl0", "cl1", "ep0", "ep1", "em0", "em1",
                "qk0", "qk1", "qk2", "qk3", "km32_0", "km32_1", "kt0", "kt1", "lrt"]
    KEYS_HDN = [f"hdn{i}" for i in range(32)]
    KEYS_STG = ["stg0", "stg1"]
    KEYS_DE = [f"t32_{i}_{p}" for i in range(4) for p in range(2)] + \
              [f"osq{i}" for i in range(2)] + [f"ro{i}" for i in range(2)] + ["t1"]
    KEYS_G = [f"sga{i}" for i in range(2)] + [f"sgb{i}" for i in range(2)] + \
             [f"tm1_{i}" for i in range(2)] + [f"tm2_{i}" for i in range(2)]
    KEYS_R32 = [f"r32_{i}" for i in range(4)]
    KEYS_HN = [f"hn{i}" for i in range(4)]
    KEYS_MT = [f"mt{i}" for i in range(8)]
    KEYS_OUT = ["outb0", "outb1"]

    NB = 8
    PS = [nc.alloc_psum_tensor(f"ps{i}", [128, 512], F32) for i in range(NB)]
    bank_ctr = [0]

    def nbank():
        b = bank_ctr[0] % NB
        bank_ctr[0] += 1
        return b

    es = ExitStack()
    sems = {}

    def sem(name):
        if name not in sems:
            sems[name] = es.enter_context(nc.semaphore(name))
        return sems[name]

    for e in COMPUTE:
        sem("e_" + e)

    def mm(out_, lhsT, rhs, start, stop, reads, writes, **kw):
        P.add(PE, lambda e: e.matmul(out_, lhsT, rhs, start=start, stop=stop, **kw), reads, writes)

    def tr(out_, in_, reads, writes):
        P.add(PE, lambda e: e.transpose(out_, in_, IDENT[:, :]), list(reads) + ["ident"], writes)

    def act(out_, in_, func, reads, writes, **kw):
        P.add(ACT, lambda e: e.activation(out_, in_, func, **kw), reads, writes)

    def cp(eng, out_, in_, reads, writes):
        if eng == ACT:
            P.add(ACT, lambda e: e.copy(out_, in_), reads, writes)
        else:
            P.add(eng, lambda e: e.tensor_copy(out_, in_), reads, writes)

    def tt(eng, out_, in0, in1, op, reads, writes):
        P.add(eng, lambda e: e.tensor_tensor(out_, in0, in1, op), reads, writes)

    def ts(eng, out_, in0, s1, s2, op0, op1, reads, writes):
        if op1 is None:
            P.add(eng, lambda e: e.tensor_scalar(out_, in0, s1, None, op0), reads, writes)
        else:
            P.add(eng, lambda e: e.tensor_scalar(out_, in0, s1, s2, op0, op1), reads, writes)

    def stt(eng, out_, in0, scalar, in1, op0, op1, reads, writes):
        P.add(eng, lambda e: e.scalar_tensor_tensor(out_, in0, scalar, in1, op0, op1), reads, writes)

    def dma(out_, in_, reads, writes, semname, eng=SP):
        P.add(eng, lambda e: e.dma_start(out_, in_), reads, writes, dma_sem=semname)
        sem(semname)

    def memset(eng, ap, val, writes):
        P.add(eng, lambda e: e.memset(ap, val), (), writes)

    dma(CST[:, :], cst32_d[:, :], [], ["cst"], "d_cst")
    dma(STG[0][:, 0:NCB], cstb_d[:, :], [], ["stg0"], "d_stg0")
    dma(HB[0][:, :, :], xmeta.rearrange("(jb p) d -> p jb d", p=128), [], [f"h0_{i}" for i in range(4)], "d_x0")
    cp(DVE, IDENT[:, :], STG[0][:, B_ID:B_ID + 128], ["stg0"], ["ident"])
    cp(DVE, BANDP[:, :, :], STG[0][:, B_BP:B_BP + 512].rearrange("p (g t) -> p g t", t=128), ["stg0"], ["bandp"])
    cp(DVE, BANDC[:, :, :], STG[0][:, B_BC:B_BC + 512].rearrange("p (g t) -> p g t", t=128), ["stg0"], ["bandc"])
    ts(DVE, NEGB[:, :], CST[:, C_BG:C_BG + 2], -1.0, None, ALU.mult, None, ["cst"], ["negb"])
    memset(DVE, SX[:, :, :, :, :], 0.0, [f"sx{i}{c}" for i in range(9) for c in "ab"])
    memset(DVE, KTE[:, :, :], 0.0, ["kte"])
    memset(DVE, KTO[:, :, :], 0.0, ["kto"])
    stg_i = [0]
    cast_i = [0]

    def prep_piece(src_ap, nk, ncol, scale_col, dst_sb=None, piece=None, k0=0):
        i = stg_i[0] % 2
        stg_i[0] += 1
        st = STG[i]
        stv = st[:, 0:nk * ncol].rearrange("p (k n) -> p k n", n=ncol)
        dma(stv, src_ap, [], [f"stg{i}"], f"d_stg{i}")
        if dst_sb is not None:
            dstv = dst_sb
            dkeys = ["wres"]
        else:
            slot = piece % NSLOTS
            dstv = RING[slot][:, 0:nk * ncol].rearrange("p (k n) -> p k n", n=ncol)
            dkeys = [f"ring{slot}"]
        if scale_col is None:
            half_n = nk // 2
            cp(DVE, dstv[:, 0:half_n, :], stv[:, 0:half_n, :], [f"stg{i}"], dkeys)
            cp(ACT, dstv[:, half_n:nk, :], stv[:, half_n:nk, :], [f"stg{i}"], dkeys)
        else:
            for k in range(nk):
                eng = (DVE, ACT)[cast_i[0] % 2]
                cast_i[0] += 1
                sc_ap = CST[:, scale_col + k0 + k:scale_col + k0 + k + 1]
                if eng == ACT:
                    act(dstv[:, k, :], stv[:, k, :], AF.Copy, [f"stg{i}", "cst"], dkeys, scale=sc_ap)
                else:
                    ts(eng, dstv[:, k, :], stv[:, k, :], sc_ap, None, ALU.mult, None, [f"stg{i}", "cst"], dkeys)
        if dst_sb is None:
            slot = piece % NSLOTS
            dma(wsc[piece, :, 0:nk * ncol], RING[slot][:, 0:nk * ncol], dkeys, [f"wsc{piece}"], f"d_ring{slot}", eng=ACT)

    prep_piece(w_lr_d[:, :, :], 8, 16, None, dst_sb=WLR[:, :, :])
    prep_piece(w_pm_d[:, :, :], 4, 128, None, dst_sb=WPM[:, :, :])
    i_ = stg_i[0] % 2
    stg_i[0] += 1
    dma(STG[i_][0:16, 0:256], w_gu_d[:, :], [], [f"stg{i_}"], f"d_stg{i_}")
    cp(DVE, WGU[:, :], STG[i_][0:16, 0:256], [f"stg{i_}"], ["wres"])

    CAST_LA = 8
    cast_next = [0]

    def emit_cast(extra_reads=()):
        pc = cast_next[0]
        if pc == 7:
            pc = 8
        if pc >= NPIECE:
            return
        cast_next[0] = pc + 1
        sem(f"d_sw{pc}")
        P.add(POOL, lambda e, pc=pc: e.dma_start(wsc[pc].rearrange("p (a n) -> p a n", n=1024),
                                                 wall_d[pc].rearrange("p (a n) -> p a n", n=1024)),
              list(extra_reads), [f"wsc{pc}"], dma_sem=f"d_sw{pc}")

    for _ in range(CAST_LA - 3):
        emit_cast()
    prep_piece(wall_d[7].rearrange("p (k n) -> p k n", n=1024), 4, 1024, C_GNP, piece=7)

    stream = {"next_load": 0, "seq": []}

    def plan_stream():
        seq = [0, 1, 2]
        for _ in range(n_seq * n_tiles):
            seq.extend(range(NPIECE))
        stream["seq"] = seq

    plan_stream()
    PIECE_LEN = {4: 4096, 5: 4096}

    def issue_load():
        n = stream["next_load"]
        if n >= len(stream["seq"]):
            return
        pc = stream["seq"][n]
        slot = n % NSLOTS
        ln = PIECE_LEN.get(pc, 4096)
        dma(RING[slot][:, 0:ln], wsc[pc, :, 0:ln], [f"wsc{pc}"], [f"ring{slot}", f"pace{n}"], f"d_ring{slot}")
        stream["next_load"] = n + 1
        emit_cast([f"pace{n}"])

    use_ctr = [0]

    def next_piece(expect):
        n = use_ctr[0]
        assert stream["seq"][n] == expect, (n, stream["seq"][n], expect)
        use_ctr[0] += 1
        slot = n % NSLOTS
        return RING[slot], f"ring{slot}"

    def done_piece():
        issue_load()

    P.alias(KEYS_STG, KEYS_GLA + KEYS_HDN)
    for _ in range(NSLOTS):
        issue_load()

    gc = [0]
    cur_s = [None]
    sctr = [0]
    UTK = [f"ut{k}" for k in range(8)]

    def hkeys(hb):
        return [f"h{hb}_{i}" for i in range(4)]

    def norm_a(hb, per_jb=False):
        Hh = HB[hb]
        for jb in range(4):
            act(UT[:, 2 * jb:2 * jb + 2, :].rearrange("p a t -> p (a t)"), Hh[:, jb, :], AF.Square,
                [f"h{hb}_{jb}"], [f"ut{2 * jb}", f"ut{2 * jb + 1}", f"ss{jb}"], accum_out=SS[:, jb:jb + 1])
            if per_jb:
                act(RS[:, jb:jb + 1], SS[:, jb:jb + 1], AF.Ln, [f"ss{jb}", "cst"], [f"rs{jb}"],
                    scale=1.0 / D, bias=CST[:, C_EPS:C_EPS + 1])
                act(RS[:, jb:jb + 1], RS[:, jb:jb + 1], AF.Exp, [f"rs{jb}"], [f"rs{jb}"], scale=-0.5)
                ts(DVE, HN[:, jb, :], Hh[:, jb, :], RS[:, jb:jb + 1], None, ALU.mult, None,
                   [f"h{hb}_{jb}", f"rs{jb}"], [f"hn{jb}"])
        if per_jb:
            return
        RSK = [f"rs{i}" for i in range(4)]
        act(RS[:, :], SS[:, :], AF.Ln, [f"ss{i}" for i in range(4)] + ["cst"], RSK,
            scale=1.0 / D, bias=CST[:, C_EPS:C_EPS + 1])
        act(RS[:, :], RS[:, :], AF.Exp, RSK, RSK, scale=-0.5)
        for jb in range(4):
            ts(DVE, HN[:, jb, :], Hh[:, jb, :], RS[:, jb:jb + 1], None, ALU.mult, None,
               [f"h{hb}_{jb}", f"rs{jb}"], [f"hn{jb}"])

    def norm_b(gcol, banks=None):
        for kc in range(8):
            b = nbank() if banks is None else banks[kc % len(banks)]
            pv = PS[b][:, :].bitcast(BF16)
            for jb in range(4):
                tr(pv[:, jb * 128:(jb + 1) * 128], HN[:, jb, kc * 128:(kc + 1) * 128],
                   [f"hn{jb}"], [f"ps{b}"])
            g_ap = CST[:, gcol + kc:gcol + kc + 1]
            if kc % 2:
                act(UT[:, kc, :], pv[:, 0:512], AF.Copy, [f"ps{b}", "cst"], [f"ut{kc}"], scale=g_ap)
            else:
                ts(DVE, UT[:, kc, :], pv[:, 0:512], g_ap, None, ALU.mult, None, [f"ps{b}", "cst"], [f"ut{kc}"])

    def load_x(kind, s, j, hb):
        src = xmeta if kind == "meta" else x[s, j * T:(j + 1) * T, :]
        dma(HB[hb][:, :, :], src.rearrange("(jb p) d -> p jb d", p=128), [], hkeys(hb), f"d_x{hb}")

    def proj_fm(w_ap, wkey, ncols_lo, evac):
        b = nbank()
        for kc in range(8):
            mm(PS[b][:, :], w_ap[:, kc, ncols_lo:ncols_lo + 128], UT[:, kc, :], kc == 0, kc == 7,
               [wkey, f"ut{kc}"], [f"ps{b}"])
        evac(b)

    def proj_tm(w_ap, wkey, jb, evac):
        b = nbank()
        for kc in range(8):
            mm(PS[b][:, :], UT[:, kc, jb * 128:(jb + 1) * 128], w_ap[:, kc, :], kc == 0, kc == 7,
               [wkey, f"ut{kc}"], [f"ps{b}"])
        evac(b)

    def emit_body(kind, s, j, hb, nxt):
        meta = kind == "meta"
        Hh = HB[hb]
        if meta:
            load_x(*nxt)
        P.alias(KEYS_HDN, KEYS_GLA)

        b = nbank()
        for kc in range(8):
            mm(PS[b][0:16, :], WLR[:, kc, :], UT[:, kc, :], kc == 0, kc == 7, ["wres", f"ut{kc}"], [f"ps{b}"])
        cp(ACT, LRT[:, :], PS[b][0:16, :], [f"ps{b}"], ["lrt"])
        for pr in range(2):
            b = nbank()
            mm(PS[b][:, :], WGU[:, pr * 128:(pr + 1) * 128], LRT[:, :], True, True, ["wres", "lrt"], [f"ps{b}"])
            act(E1L[:, pr, :], PS[b][:, :], AF.Exp, [f"ps{b}", "negb"], [f"e1l{pr}"], scale=-1.0, bias=NEGB[:, pr:pr + 1])
        W, wk = next_piece(0)
        Wv = W[:, :].rearrange("p (k n) -> p k n", n=512)
        for n in (2, 3, 0, 1):
            if meta and n < 2:
                continue
            proj_fm(Wv, wk, n * 128, lambda b, n=n: act(QK32[:, n, :], PS[b][:, :], AF.Copy, [f"ps{b}"], [f"qk{n}"],
                                                       scale=(0.125 if n < 2 else 1.0)))
        done_piece()
        for pr in range(2):
            act(E1L[:, pr, :], E1L[:, pr, :], AF.Ln, [f"e1l{pr}"], [f"e1l{pr}"], bias=1.0)
            P.add(DVE, lambda e, pr=pr: e.tensor_tensor_scan(CL[:, pr, :], CST[:, C_CM:C_CM + 512], E1L[:, pr, :],
                                                             0.0, ALU.mult, ALU.add),
                  [f"e1l{pr}", "cst"], [f"cl{pr}"])
            act(EP[:, pr, :], CL[:, pr, :], AF.Exp, [f"cl{pr}"], [f"ep{pr}"], scale=-1.0 / 16.0)
            act(EM[:, pr, :], CL[:, pr, :], AF.Exp, [f"cl{pr}"], [f"em{pr}"], scale=1.0 / 16.0)
        cp(DVE, DF[:, :, :], EP[:, :, :].rearrange("p a (c t) -> p a c t", t=64)[:, :, :, 63],
           ["ep0", "ep1"], ["df"])
        if pending_i[0] is not None:
            pending_i[0]()
            pending_i[0] = None
        P.alias(KEYS_R32 + KEYS_G + KEYS_OUT, KEYS_DE)
        if meta:
            pass
        elif j == 0:
            cp(POOL, ATOK[:, 0, :], AMETA[:, :], ["ameta"], ["atok0"])
        else:
            cp(POOL, ATOK[:, 0, :], ATOK[:, 4, :], ["atok4"], ["atok0"])
        W, wk = next_piece(1)
        Wv = W[:, :].rearrange("p (k n) -> p k n", n=512)
        for jb in range(4):
            proj_tm(Wv, wk, jb, lambda b, jb=jb: cp(ACT, ATOK[:, jb + 1, :], PS[b][:, :],
                                                   [f"ps{b}"], [f"atok{jb + 1}"]))
        done_piece()
        if meta:
            cp(POOL, AMETA[:, :], ATOK[:, 4, :], ["atok4"], ["ameta"])
        for pr in range(2):
            tt(DVE, KM32[:, pr, :], QK32[:, 2 + pr, :], EM[:, pr, :], ALU.mult,
               [f"qk{2 + pr}", f"em{pr}"], [f"km32_{pr}"])
            tt(DVE, KT[:, pr, :].rearrange("p (c t) -> p c t", t=64),
               KM32[:, pr, :].rearrange("p (c t) -> p c t", t=64),
               DF[:, pr, :].unsqueeze(2).to_broadcast([128, 8, 64]), ALU.mult,
               [f"km32_{pr}", "df"], [f"kt{pr}"])
        W, wk = next_piece(2)
        Wv = W[:, :].rearrange("p (k n) -> p k n", n=512)
        for jb in range(4):
            proj_tm(Wv, wk, jb, lambda b, jb=jb: cp(ACT, VTOK[:, jb, :], PS[b][:, :],
                                                   [f"ps{b}"], [f"vtok{jb}"]))
        done_piece()
        for pr in range(2):
            b = nbank()
            pv = PS[b][:, :].bitcast(BF16)
            for jb in range(4):
                tr(pv[:, jb * 128:(jb + 1) * 128], KT[:, pr, jb * 128:(jb + 1) * 128], [f"kt{pr}"], [f"ps{b}"])
            src_v = pv[:, 0:512].rearrange("p (jb n) -> p jb n", n=128)
            cp(DVE, KTE[0:64, :, pr * 128:(pr + 1) * 128], src_v[0:64, :, :], [f"ps{b}"], ["kte"])
            cp(ACT, KTO[64:128, :, pr * 128:(pr + 1) * 128], src_v[64:128, :, :], [f"ps{b}"], ["kto"])

        ubank = {}
        for cp2 in range(4):
            b = nbank()
            for cc in range(2):
                c = cp2 * 2 + cc
                ubank[c] = (b, cc)
                jb, cpar = c // 2, c % 2
                KX = KTE if cpar == 0 else KTO
                for h in range(4):
                    pr, par = h // 2, h % 2
                    o_ap = PS[b][64 * par:64 * par + 64, cc * 256 + pr * 128: cc * 256 + (pr + 1) * 128]
                    mm(o_ap, KX[:, jb, h * 64:(h + 1) * 64], VTOK[:, jb, h * 128:(h + 1) * 128], True, True,
                       ["kte" if cpar == 0 else "kto", f"vtok{jb}"], [f"ps{b}"])
        if meta:
            memset(DVE, SB2[2][:, :, :], 0.0, ["s2p0", "s2p1"])
            cur_s[0] = (SB2[2], "s2p")
        elif j == 0:
            cur_s[0] = (SMETA, "smetap")
            sl = gc[0] % 9
            cp(POOL, SX[0:64, sl, :, 0, :], SMETA[0:64, :, :], ["smetap0", "smetap1"], [f"sx{sl}a"])
            cp(ACT, SX[64:128, sl, :, 1, :], SMETA[64:128, :, :], ["smetap0", "smetap1"], [f"sx{sl}b"])
        slots = []
        for c in range(8):
            slots.append(gc[0] % 9)
            ub, ucc = ubank[c]
            S_old, ko = cur_s[0]
            nb_ = (sctr[0]) % 3
            sctr[0] += 1
            S_new, kn = SB2[nb_], f"s{nb_}p"
            for pr in range(2):
                stt(DVE, S_new[:, pr, :], S_old[:, pr, :], DF[:, pr, c:c + 1],
                    PS[ub][:, ucc * 256 + pr * 128:ucc * 256 + (pr + 1) * 128], ALU.mult, ALU.add,
                    [f"{ko}{pr}", "df", f"ps{ub}"], [f"{kn}{pr}"])
            cur_s[0] = (S_new, kn)
            if not meta:
                sl = (gc[0] + 1) % 9
                cp(POOL, SX[0:64, sl, :, 0, :], S_new[0:64, :, :], [f"{kn}0", f"{kn}1"], [f"sx{sl}a"])
                cp(ACT, SX[64:128, sl, :, 1, :], S_new[64:128, :, :], [f"{kn}0", f"{kn}1"], [f"sx{sl}b"])
                gc[0] += 1
        if not meta:
            for pr in range(2):
                act(KM[:, pr, :], KM32[:, pr, :], AF.Copy, [f"km32_{pr}"], [f"km{pr}"])
                tt(POOL, KP[:, pr, :], QK32[:, 2 + pr, :], EP[:, pr, :], ALU.mult,
                   [f"qk{2 + pr}", f"ep{pr}"], [f"kp{pr}"])
                tt(DVE, QP[:, pr, :], QK32[:, pr, :], EP[:, pr, :], ALU.mult,
                   [f"qk{pr}", f"ep{pr}"], [f"qp{pr}"])
                tt(DVE, QM[:, pr, :], QK32[:, pr, :], EM[:, pr, :], ALU.mult,
                   [f"qk{pr}", f"em{pr}"], [f"qm{pr}"])
        if meta:
            S_fin, kf = cur_s[0]
            cp(DVE, SMETA[:, :, :], S_fin[:, :, :], [f"{kf}0", f"{kf}1"], ["smetap0", "smetap1"])
            norm_a(nxt[3])
            norm_b(C_G1P)
            return

        for g in range(4):
            b = nbank()
            for jb in range(4):
                mm(PS[b][:, jb * 128:(jb + 1) * 128], ATOK[:, jb, g * 128:(g + 1) * 128], BANDP[:, g, :], True, False,
                   [f"atok{jb}", "bandp"], [f"ps{b}"])
                mm(PS[b][:, jb * 128:(jb + 1) * 128], ATOK[:, jb + 1, g * 128:(g + 1) * 128], BANDC[:, g, :], False, True,
                   [f"atok{jb + 1}", "bandc"], [f"ps{b}"])
            cp(ACT, PT[:, g, :], PS[b][:, :], [f"ps{b}"], [f"pt{g}"])

        W, wk = next_piece(3)
        Wv = W[:, :].rearrange("p (k n) -> p k n", n=512)
        for n in range(4):
            proj_fm(Wv, wk, n * 128, lambda b, n=n: act(SR[:, n, :], PS[b][:, :], AF.Silu, [f"ps{b}"], [f"sr{n}"]))
        done_piece()

        act(WARM[:, 0:1], CST[:, C_EPS:C_EPS + 1], AF.Ln, ["cst"], ["warm"])
        for g in range(4):
            b2 = nbank()
            mm(PS[b2][:, :], WPM[:, g, :], PT[:, g, :], True, True, ["wres", f"pt{g}"], [f"ps{b2}"])
            sc_ap = CST[:, C_PSC + g:C_PSC + g + 1]
            act(P2T[:, g, :], PS[b2][:, :], AF.Copy, [f"ps{b2}", "cst"], [f"p2t{g}"], scale=sc_ap)

        def e_scores(jb):
            bx = [nbank(), nbank()]
            for h in range(4):
                pr, par = h // 2, h % 2
                rows = slice(64 * par, 64 * par + 64)
                tok = slice(jb * 128, (jb + 1) * 128)
                mm(PS[bx[par]][:, pr * 128:(pr + 1) * 128], KM[rows, pr, tok], QP[rows, pr, tok], True, True,
                   [f"km{pr}", f"qp{pr}"], [f"ps{bx[par]}"])
                mm(PS[bx[par]][:, 256 + pr * 128:256 + (pr + 1) * 128], KP[rows, pr, tok], QM[rows, pr, tok], True, True,
                   [f"kp{pr}", f"qm{pr}"], [f"ps{bx[par]}"])
            for par in range(2):
                tt(DVE, T32B[:, jb, par, :], PS[bx[par]][:, :], CST[:, C_MCAT:C_MCAT + 512], ALU.mult,
                   [f"ps{bx[par]}", "cst"], [f"t32_{jb}_{par}"])

        obank = {}

        def e_out(jb):
            bo = nbank()
            obank[jb] = bo
            for h in range(4):
                pr, par = h // 2, h % 2
                for cc in range(2):
                    c = jb * 2 + cc
                    sl = slots[c]
                    o_ap = PS[bo][:, h * 128 + cc * 64:h * 128 + (cc + 1) * 64]
                    mm(o_ap, SX[:, sl, pr, par, :], QP[:, pr, c * 64:(c + 1) * 64], True, False,
                       [f"sx{sl}a", f"sx{sl}b", f"qp{pr}"], [f"ps{bo}"])
                    for ab in range(2):
                        c0 = ab * 256 + pr * 128 + cc * 64
                        mm(o_ap, VTOK[:, jb, h * 128:(h + 1) * 128], T32B[:, jb, par, c0:c0 + 64], False, ab == 1,
                           [f"vtok{jb}", f"t32_{jb}_{par}"], [f"ps{bo}"])
            act(OSQ[jb % 2][:, :], PS[bo][:, :], AF.Square, [f"ps{bo}"], [f"osq{jb % 2}"])

        def e_norm(jb):
            i2 = jb % 2
            bo = obank[jb]
            bm = nbank()
            mm(PS[bm][:, :], ONESB, OSQ[i2][:, :], True, True, ["cst", f"osq{i2}"], [f"ps{bm}"])
            act(RO[i2][:, :], PS[bm][:, :], AF.Ln, [f"ps{bm}", "cst"], [f"ro{i2}"], scale=1.0 / 128.0,
                bias=CST[:, C_EPS:C_EPS + 1])
            act(RO[i2][:, :], RO[i2][:, :], AF.Exp, [f"ro{i2}"], [f"ro{i2}"], scale=-0.5)
            tt(DVE, T1[:, :], PS[bo][:, :], RO[i2][:, :], ALU.mult, [f"ps{bo}", f"ro{i2}"], ["t1"])
            tt(POOL, OG[:, :, jb * 128:(jb + 1) * 128], T1[:, :].rearrange("p (h t) -> p h t", t=128),
               SR[:, :, jb * 128:(jb + 1) * 128], ALU.mult, ["t1"] + [f"sr{n}" for n in range(4)], [f"og{jb}"])

        e_scores(0)
        e_scores(1)
        e_out(0)
        e_scores(2)
        e_out(1)
        e_norm(0)
        e_scores(3)
        e_out(2)
        e_norm(1)
        e_out(3)
        e_norm(2)
        e_norm(3)

        P.alias(KEYS_DE, KEYS_G)
        P.alias(KEYS_HN, KEYS_MT)
        OGK = [f"og{i}" for i in range(4)]
        Wpo_v = Wgo_v = kpo = kgo = None
        for half in range(2):
            Wga, kga = next_piece(4 + 4 * half)
            Wgb, kgb = next_piece(5 + 4 * half)
            if half == 0:
                Wpo, kpo = next_piece(6)
                Wgo, kgo = next_piece(7)
                Wpo_v = Wpo[:, :].rearrange("p (g n) -> p g n", n=1024)
                Wgo_v = Wgo[:, :].rearrange("p (g n) -> p g n", n=1024)
            Wga_v = Wga[:, :].rearrange("p (k n) -> p k n", n=512)
            Wgb_v = Wgb[:, :].rearrange("p (k n) -> p k n", n=512)
            for dl in range(4):
                dch = half * 4 + dl
                i2 = dch % 2
                proj_fm(Wga_v, kga, dl * 128, lambda b: act(SGA[i2][:, :], PS[b][:, :], AF.Sigmoid,
                                                            [f"ps{b}"], [f"sga{i2}"]))
                proj_fm(Wgb_v, kgb, dl * 128, lambda b: act(SGB[i2][:, :], PS[b][:, :], AF.Sigmoid,
                                                            [f"ps{b}"], [f"sgb{i2}"]))
                ba = nbank()
                for g in range(4):
                    mm(PS[ba][:, :], Wpo_v[:, g, dch * 128:(dch + 1) * 128], P2T[:, g, :], g == 0, g == 3,
                       [kpo, f"p2t{g}"], [f"ps{ba}"])
                tt(DVE, TM1[i2][:, :], PS[ba][:, :], SGA[i2][:, :], ALU.mult, [f"ps{ba}", f"sga{i2}"], [f"tm1_{i2}"])
                bb = nbank()
                for g in range(4):
                    mm(PS[bb][:, :], Wgo_v[:, g, dch * 128:(dch + 1) * 128], OG[:, g, :], g == 0, g == 3,
                       [kgo] + OGK, [f"ps{bb}"])
                tt(DVE, TM2[i2][:, :], PS[bb][:, :], SGB[i2][:, :], ALU.mult, [f"ps{bb}", f"sgb{i2}"], [f"tm2_{i2}"])
                tt(DVE if dch == 7 else POOL, MT[:, dch, :], TM1[i2][:, :], TM2[i2][:, :], ALU.add, [f"tm1_{i2}", f"tm2_{i2}"], [f"mt{dch}"])
            for _ in range(2 if half == 0 else 4):
                done_piece()
        act(WARM[:, 0:1], CST[:, C_EPS:C_EPS + 1], AF.Ln, ["cst"], ["warm"])
        for half in range(2):
            W, wk = next_piece(10 + half)
            Wv = W[:, :].rearrange("p (k n) -> p k n", n=512)
            for jb in range(4):
                b = nbank()
                for dch in range(8):
                    mm(PS[b][:, :], MT[:, dch, jb * 128:(jb + 1) * 128], Wv[:, dch, :], dch == 0, dch == 7,
                       [wk, f"mt{dch}"], [f"ps{b}"])
                tt(DVE, Hh[:, jb, half * 512:(half + 1) * 512], PS[b][:, :], Hh[:, jb, half * 512:(half + 1) * 512],
                   ALU.add, [f"ps{b}", f"h{hb}_{jb}"], [f"h{hb}_{jb}"])
            done_piece()

        P.alias(KEYS_MT, KEYS_HN)
        P.alias(KEYS_GLA, KEYS_HDN)
        P.alias(KEYS_G, KEYS_R32 + KEYS_OUT)
        norm_a(hb, per_jb=True)
        norm_b(C_G2P)
        if nxt is not None:
            load_x(*nxt)
        for q in range(8):
            W, wk = next_piece(12 + q)
            Wv = W[:, :].rearrange("p (k n) -> p k n", n=512)
            for fl in range(4):
                fch = q * 4 + fl
                i4 = fch % 4
                b = nbank()
                for kc in range(8):
                    mm(PS[b][:, :], Wv[:, kc, fl * 128:(fl + 1) * 128], UT[:, kc, :], kc == 0, kc == 7,
                       [wk, f"ut{kc}"], [f"ps{b}"])
                act(R32[i4][:, :], PS[b][:, :], AF.Relu, [f"ps{b}"], [f"r32_{i4}"])
                tt(POOL if fch % 2 else DVE, HDN[:, fch, :], R32[i4][:, :], R32[i4][:, :], ALU.mult,
                   [f"r32_{i4}"], [f"hdn{fch}"])
            done_piece()
        for half in range(2):
            accb = [nbank() for _ in range(4)]
            for g4 in range(4):
                W, wk = next_piece(20 + half * 4 + g4)
                Wv = W[:, :].rearrange("p (k n) -> p k n", n=512)
                for fl in range(8):
                    fch = g4 * 8 + fl
                    for jb in range(4):
                        mm(PS[accb[jb]][:, :], HDN[:, fch, jb * 128:(jb + 1) * 128], Wv[:, fl, :],
                           fch == 0, fch == 31, [wk, f"hdn{fch}"], [f"ps{accb[jb]}"])
                done_piece()
            if half == 1 and nxt is not None:
                norm_a(nxt[3])
                norm_b(C_G1P, banks=[b for b in range(NB) if b not in accb])
            for jb in range(4):
                tt(DVE, Hh[:, jb, half * 512:(half + 1) * 512], PS[accb[jb]][:, :],
                   Hh[:, jb, half * 512:(half + 1) * 512], ALU.add, [f"ps{accb[jb]}", f"h{hb}_{jb}"], [f"h{hb}_{jb}"])

        def stage_i():
            for jb in range(4):
                act(JNK[:, jb % 2, :], Hh[:, jb, :], AF.Square, [f"h{hb}_{jb}"], [f"r32_{jb % 2}", f"ss{jb}"],
                    accum_out=SS[:, jb:jb + 1])
            act(RS[:, :], SS[:, :], AF.Ln, [f"ss{i}" for i in range(4)] + ["cst"], [f"rs{i}" for i in range(4)],
                scale=1.0 / D, bias=CST[:, C_EPS:C_EPS + 1])
            act(RS[:, :], RS[:, :], AF.Exp, [f"rs{i}" for i in range(4)], [f"rs{i}" for i in range(4)], scale=-0.5)
            for jb in range(4):
                ob = jb % 2
                act(OUTB[:, ob, :], Hh[:, jb, :], AF.Copy, [f"h{hb}_{jb}", f"rs{jb}"], [f"outb{ob}"],
                    scale=RS[:, jb:jb + 1])
                tt(POOL, OUTB[:, ob, :], OUTB[:, ob, :], CST[:, C_G3:C_G3 + D], ALU.mult,
                   [f"outb{ob}", "cst"], [f"outb{ob}"])
                dma(out[s, j * T + jb * 128:j * T + (jb + 1) * 128, :], OUTB[:, ob, :], [f"outb{ob}"],
                    [f"out_{s}_{j}_{jb}"], f"d_out{ob}")

        if nxt is None:
            stage_i()
        else:
            pending_i[0] = stage_i

    pending_i = [None]
    tiles = [("meta", 0, 0)] + [("main", s, j) for s in range(n_seq) for j in range(n_tiles)]
    norm_a(0)
    norm_b(C_G1P)
    for i, (kind, s, j) in enumerate(tiles):
        hb = i % 2
        nxt = None
        if i + 1 < len(tiles):
            nk, ns, nj = tiles[i + 1]
            nxt = (nk, ns, nj, (i + 1) % 2)
        emit_body(kind, s, j, hb, nxt)

    resolve(P)
    final_dma = [(sems[k], v) for k, v in P.dma_count.items() if k.startswith("d_out")]
    with es:
        with nc.Block() as block:
            def emit_engine(eng_name):
                def body(e):
                    for op in P.by_eng[eng_name]:
                        for w in op.waits:
                            if w[0] == "dma":
                                e.wait_ge(sems[w[1]], w[2])
                            else:
                                d = P.ops[w[2]]
                                e.wait_ge(sems["e_" + d.eng], d.rank)
                        ins = op.fn(e)
                        if op.dma_sem is not None:
                            ins.then_inc(sems[op.dma_sem], 16)
                        elif op.target:
                            ins.then_inc(sems["e_" + eng_name], 1)
                    if eng_name == SP:
                        for sm, v in final_dma:
                            e.wait_ge(sm, v)
                return body

            block.tensor(emit_engine(PE))
            block.scalar(emit_engine(ACT))
            block.vector(emit_engine(DVE))
            block.gpsimd(emit_engine(POOL))
            block.sync(emit_engine(SP))
    return nc


def resolve(P):
    for op in P.ops:
        E = op.eng
        lst = P.by_eng[E]
        prev = lst[op.pos - 1] if op.pos > 0 else None
        clock = dict(prev.clock) if prev is not None else {}
        waits = []
        for d_idx in sorted(op.deps):
            kind = op.deps[d_idx]
            d = P.ops[d_idx]
            if d.dma_sem is not None:
                key = ("dma", d.dma_sem)
                if clock.get(key, 0) >= d.dma_val:
                    continue
                waits.append(("dma", d.dma_sem, d.dma_val))
                clock[key] = d.dma_val
                continue
            if d.eng == SP:
                continue
            if d.eng == E and E == PE:
                continue
            if clock.get(d.eng, -1) >= d.pos:
                continue
            waits.append(("eng", d.eng, d_idx))
            d.target = True
            for k, v in d.clock.items():
                if clock.get(k, -1 if not isinstance(k, tuple) else 0) < v:
                    clock[k] = v
            clock[d.eng] = max(clock.get(d.eng, -1), d.pos)
        op.waits = waits
        op.clock = clock
    for e in COMPUTE:
        r = 0
        for op in P.by_eng[e]:
            if op.target:
                r += 1
                op.rank = r


def _host_consts(norm_mix, norm_ffn, norm_final, pool_scale, gla_norm, b_gate):
    c = np.zeros((128, NC32), np.float32)
    s_idx = np.arange(128)[:, None]
    t_idx = np.arange(128)[None, :]
    same = (s_idx // 64) == (t_idx // 64)
    m1 = (same & (s_idx <= t_idx)).astype(np.float32)
    m2 = (same & (s_idx > t_idx)).astype(np.float32)
    c[:, C_MCAT:C_MCAT + 512] = np.concatenate([m1, m1, m2, m2], axis=1)
    cm = np.ones(512, np.float32)
    cm[::64] = 0.0
    c[:, C_CM:C_CM + 512] = cm[None, :]
    c[:, C_ONES:C_ONES + 128] = np.array([0x3F803F80], np.uint32).view(np.float32)[0]
    c[:, C_G3:C_G3 + D] = norm_final[None, :]
    c[:, C_G1P:C_G1P + 8] = norm_mix.reshape(8, 128).T
    c[:, C_G2P:C_G2P + 8] = norm_ffn.reshape(8, 128).T
    c[:, C_PSC:C_PSC + 4] = pool_scale.reshape(4, 128).T
    c[:, C_GNP:C_GNP + 4] = gla_norm.reshape(4, 128).T
    c[:, C_BG:C_BG + 2] = b_gate.reshape(2, 128).T
    c[:, C_EPS] = EPS
    cb = np.zeros((128, NCB), np.float32)
    cb[:, B_ID:B_ID + 128] = np.eye(128, dtype=np.float32)
    for g, w in enumerate((2, 4, 8, 16)):
        dlt = t_idx - s_idx
        cur = ((dlt >= 0) & (dlt <= w - 1)).astype(np.float32) / w - (dlt == 0).astype(np.float32)
        dlp = t_idx + 128 - s_idx
        prv = ((dlp >= 0) & (dlp <= w - 1)).astype(np.float32) / w
        cb[:, B_BC + g * 128:B_BC + (g + 1) * 128] = cur
        cb[:, B_BP + g * 128:B_BP + (g + 1) * 128] = prv
    return c, cb


def _lay(w, nk):
    return np.ascontiguousarray(w.reshape(nk, 128, w.shape[-1]).transpose(1, 0, 2))


_NC_CACHE = {}


def run(x, meta_tokens, norm_mix, w_in, w_pool_mix, pool_scale, w_pool_out, w_gate_up, b_gate,
        gla_norm, w_gla_out, w_out, norm_ffn, w_ff1, w_ff2, norm_final, n_cores=N_CORES):
    x = np.asarray(x, np.float32)
    B, L, _ = x.shape
    n_seq = B // n_cores
    n_tiles = L // T
    key = (n_seq, n_tiles)
    if key not in _NC_CACHE:
        _NC_CACHE[key] = build(n_seq, n_tiles)
    nc = _NC_CACHE[key]
    f = lambda a: np.asarray(a, np.float32)
    c32, cb = _host_consts(f(norm_mix)[0], f(norm_ffn)[0], f(norm_final), f(pool_scale)[0], f(gla_norm)[0],
                           f(b_gate)[0])
    xmeta = np.zeros((T, D), np.float32)
    xmeta[T - 16:] = f(meta_tokens)
    w_in_l = _lay(f(w_in)[0], 8)
    w_ff1_l = _lay(f(w_ff1)[0], 8)
    w_ff2_l = _lay(f(w_ff2)[0], 32)
    w_out_l = _lay(f(w_out)[0], 8)
    wall = np.empty((28, 128, 4096), np.float32)
    in_cols = {0: 512, 1: 0, 2: 1024, 3: 1536, 4: 2064, 5: 3088, 8: 2576, 9: 3600}
    for pc in range(28):
        if pc in in_cols:
            c0 = in_cols[pc]
            blk = w_in_l[:, :, c0:c0 + 512]
        elif pc == 6:
            blk = _lay(f(w_pool_out)[0], 4)
        elif pc == 7:
            blk = _lay(f(w_gla_out)[0], 4)
        elif pc in (10, 11):
            hh = pc - 10
            blk = w_out_l[:, :, hh * 512:(hh + 1) * 512]
        elif 12 <= pc < 20:
            q = pc - 12
            blk = w_ff1_l[:, :, q * 512:(q + 1) * 512]
        else:
            q = pc - 20
            hh, g4 = q // 4, q % 4
            blk = w_ff2_l[:, g4 * 8:(g4 + 1) * 8, hh * 512:(hh + 1) * 512]
        wall[pc] = blk.reshape(128, 4096)
    shared = {
        "xmeta": xmeta, "cst32": c32, "cstb": cb, "wall": wall,
        "w_lr_l": np.ascontiguousarray(w_in_l[:, :, 2048:2064]),
        "w_pm_l": np.ascontiguousarray(f(w_pool_mix)[0].transpose(1, 0, 2)),
        "w_gu": np.ascontiguousarray(f(w_gate_up)[0]),
    }
    in_maps = []
    for c in range(n_cores):
        m = dict(shared)
        m["x"] = np.ascontiguousarray(x[c * n_seq:(c + 1) * n_seq])
        in_maps.append(m)
    res = run_bass_kernel_spmd(nc, in_maps, core_ids=list(range(n_cores)))
    return np.concatenate([np.asarray(r["out"], np.float32) for r in res.results], axis=0)


def kernel(**inputs):
    return run(**inputs)
```

```python
import numpy as np
from contextlib import ExitStack
import concourse.bass as bass
import concourse.mybir as mybir
from concourse.bass_utils import run_bass_kernel_spmd

F32 = mybir.dt.float32
BF16 = mybir.dt.bfloat16
AF = mybir.ActivationFunctionType
ALU = mybir.AluOpType

D = 1024
T = 512
NSLOTS = 5
EPS = 1e-6
N_CORES = 8

C_MCAT, C_CM, C_ONES, C_G3 = 0, 512, 1024, 1152
C_G1P, C_G2P, C_PSC, C_GNP, C_BG, C_EPS = 2176, 2184, 2192, 2196, 2200, 2202
NC32 = 2208
B_ID, B_BP, B_BC = 0, 128, 640
NCB = 1152

PE, ACT, DVE, POOL, SP = "pe", "act", "dve", "pool", "sp"
COMPUTE = (PE, ACT, DVE, POOL)


class Op:
    pass


class Prog:
    def __init__(self):
        self.ops = []
        self.by_eng = {e: [] for e in (PE, ACT, DVE, POOL, SP)}
        self.last_w = {}
        self.readers = {}
        self.dma_count = {}

    def add(self, eng, fn, reads=(), writes=(), dma_sem=None):
        op = Op()
        op.eng, op.fn, op.idx = eng, fn, len(self.ops)
        op.pos = len(self.by_eng[eng])
        op.dma_sem = dma_sem
        op.target = False
        op.rank = None
        if dma_sem is not None:
            self.dma_count[dma_sem] = self.dma_count.get(dma_sem, 0) + 16
            op.dma_val = self.dma_count[dma_sem]
        deps = {}
        for k in reads:
            w = self.last_w.get(k)
            if w is not None:
                deps[w] = "raw"
        for k in writes:
            w = self.last_w.get(k)
            if w is not None and w not in deps:
                deps[w] = "waw"
            for r in self.readers.get(k, ()):
                if r not in deps:
                    deps[r] = "war"
        deps.pop(op.idx, None)
        op.deps = deps
        for k in reads:
            self.readers.setdefault(k, []).append(op.idx)
        for k in writes:
            self.last_w[k] = op.idx
            self.readers[k] = []
        self.ops.append(op)
        self.by_eng[eng].append(op)
        return op

    def alias(self, old_keys, new_keys):
        acc = []
        for k in old_keys:
            w = self.last_w.get(k)
            if w is not None:
                acc.append(w)
            acc.extend(self.readers.get(k, ()))
        for k in new_keys:
            self.last_w.pop(k, None)
            self.readers[k] = list(acc)


def build(n_seq=4, n_tiles=4, debug=False):
    nc = bass.Bass("TRN2", target_bir_lowering=False)
    P = Prog()

    def din(name, shape):
        return nc.dram_tensor(name, list(shape), F32, kind="ExternalInput").ap()

    x = din("x", [n_seq, n_tiles * T, D])
    xmeta = din("xmeta", [T, D])
    cst32_d = din("cst32", [128, NC32])
    cstb_d = din("cstb", [128, NCB])
    wall_d = din("wall", [28, 128, 4096])
    w_lr_d = din("w_lr_l", [128, 8, 16])
    w_pm_d = din("w_pm_l", [128, 4, 128])
    w_gu_d = din("w_gu", [16, 256])
    out = nc.dram_tensor("out", [n_seq, n_tiles * T, D], F32, kind="ExternalOutput").ap()
    NPIECE = 28
    wsc = nc.dram_tensor("wsc", [NPIECE, 128, 4096], BF16, kind="Internal").ap()

    off = [16512]

    def sb(name, shape, dtype, at=None):
        esz = 4 if dtype == F32 else 2
        n = 1
        for s_ in shape[1:]:
            n *= s_
        nbytes = n * esz
        if at is None:
            at_ = off[0]
            off[0] += (nbytes + 63) // 64 * 64
        else:
            at_ = at
        t = nc.alloc_sbuf_tensor_at(name, list(shape), dtype, offset=at_)
        return t, at_, nbytes

    CST, _, _ = sb("cst", [128, NC32], F32)
    IDENT, _, _ = sb("ident", [128, 128], BF16)
    ONESB = CST[:, C_ONES:C_ONES + 64].bitcast(BF16)
    BANDP, _, _ = sb("bandp", [128, 4, 128], BF16)
    BANDC, _, _ = sb("bandc", [128, 4, 128], BF16)
    WLR, _, _ = sb("wlr", [128, 8, 16], BF16)
    WPM, _, _ = sb("wpm", [128, 4, 128], BF16)
    WGU, _, _ = sb("wgu", [16, 256], BF16)
    NEGB, _, _ = sb("negb", [128, 2], F32)
    H0, h_at, _ = sb("h0", [128, 4, D], F32)
    H1, _, _ = sb("h1", [128, 4, D], F32)
    HB = [H0, H1]
    HN, hn_at, _ = sb("hn", [128, 4, D], BF16)
    MT, _, _ = sb("mt", [128, 8, T], BF16, at=hn_at)
    UT, _, _ = sb("ut", [128, 8, T], BF16)
    ATOK, _, _ = sb("atok", [128, 5, 512], BF16)
    AMETA, _, _ = sb("ameta", [128, 512], BF16)
    VTOK, _, _ = sb("vtok", [128, 4, 512], BF16)
    SR, _, _ = sb("sr", [128, 4, T], F32)
    SS, ss_at, _ = sb("ss", [128, 4], F32)
    WARM = sb("warm", [128, 1], F32, at=ss_at + 32)[0]
    RS, _, _ = sb("rs", [128, 4], F32)
    HDN, r1, _ = sb("hdn", [128, 32, T], BF16)
    E1L, _, _ = sb("e1l", [128, 2, T], F32, at=r1)
    CL, _, _ = sb("cl", [128, 2, T], F32, at=r1 + 4096)
    EP, _, _ = sb("ep", [128, 2, T], F32, at=r1 + 8192)
    EM, _, _ = sb("em", [128, 2, T], F32, at=r1 + 12288)
    QK32, _, _ = sb("qk32", [128, 4, T], F32, at=r1 + 16384)
    KM32, _, _ = sb("km32", [128, 2, T], F32, at=r1 + 24576)
    KT, _, _ = sb("kt", [128, 2, T], BF16, at=r1 + 28672)
    LRT, _, _ = sb("lrt", [16, T], BF16, at=r1 + 30720)
    STG = [sb(f"stg{i}", [128, 4096], F32, at=r1 + 16384 * i)[0] for i in range(2)]
    STG.append(sb("stg2", [128, 4096], F32, at=h_at)[0])
    STG.append(sb("stg3", [128, 4096], F32, at=hn_at)[0])
    QP, _, _ = sb("qp", [128, 2, T], BF16)
    QM, _, _ = sb("qm", [128, 2, T], BF16)
    KP, _, _ = sb("kp", [128, 2, T], BF16)
    KM, _, _ = sb("km", [128, 2, T], BF16)
    KTE, _, _ = sb("kte", [128, 4, 256], BF16)
    KTO, _, _ = sb("kto", [128, 4, 256], BF16)
    DF, _, _ = sb("df", [128, 2, 8], F32)
    SB2 = [sb(f"s{i}", [128, 2, 128], F32)[0] for i in range(3)]
    SMETA, _, _ = sb("smeta", [128, 2, 128], F32)
    SX, _, _ = sb("sx", [128, 9, 2, 2, 128], BF16)
    OG, _, _ = sb("og", [128, 4, T], BF16)
    PT, _, _ = sb("pt", [128, 4, T], BF16)
    P2T, _, _ = sb("p2t", [128, 4, T], BF16)
    R2T, r2, _ = sb("r2t", [128, 4608], F32)
    T32B = sb("t32b", [128, 4, 2, 512], BF16, at=r2)[0]
    OSQ = [sb(f"osq_{i}", [128, 512], BF16, at=r2 + 8192 + 2048 * i)[0] for i in range(2)]
    RO = [sb(f"ro_{i}", [128, 512], F32, at=r2 + 12288 + 2048 * i)[0] for i in range(2)]
    T1 = sb("t1", [128, 512], F32, at=r2 + 16384)[0]
    SGA = [sb(f"sga_{i}", [128, 512], F32, at=r2 + 2048 * i)[0] for i in range(2)]
    SGB = [sb(f"sgb_{i}", [128, 512], F32, at=r2 + 4096 + 2048 * i)[0] for i in range(2)]
    TM1 = [sb(f"tm1_{i}", [128, 512], F32, at=r2 + 8192 + 2048 * i)[0] for i in range(2)]
    TM2 = [sb(f"tm2_{i}", [128, 512], F32, at=r2 + 12288 + 2048 * i)[0] for i in range(2)]
    R32 = [sb(f"r32_{i}", [128, 512], F32, at=r2 + 2048 * i)[0] for i in range(4)]
    OUTB = sb("outb", [128, 2, D], F32, at=r2 + 8192)[0]
    JNK = sb("jnk", [128, 2, D], BF16, at=r2)[0]
    RING = [sb(f"ring{i}", [128, 4096], BF16)[0] for i in range(NSLOTS)]
    assert off[0] <= 229376, off[0]

    KEYS_GLA = ["e1l0", "e1l1", "cl0", "cl1", "ep0", "ep1", "em0", "em1",
                "qk0", "qk1", "qk2", "qk3", "km32_0", "km32_1", "kt0", "kt1", "lrt"]
    KEYS_HDN = [f"hdn{i}" for i in range(32)]
    KEYS_STG = ["stg0", "stg1"]
    KEYS_DE = [f"t32_{i}_{p}" for i in range(4) for p in range(2)] + \
              [f"osq{i}" for i in range(2)] + [f"ro{i}" for i in range(2)] + ["t1"]
    KEYS_G = [f"sga{i}" for i in range(2)] + [f"sgb{i}" for i in range(2)] + \
             [f"tm1_{i}" for i in range(2)] + [f"tm2_{i}" for i in range(2)]
    KEYS_R32 = [f"r32_{i}" for i in range(4)]
    KEYS_HN = [f"hn{i}" for i in range(4)]
    KEYS_MT = [f"mt{i}" for i in range(8)]
    KEYS_OUT = ["outb0", "outb1"]

    NB = 8
    PS = [nc.alloc_psum_tensor(f"ps{i}", [128, 512], F32) for i in range(NB)]
    bank_ctr = [0]

    def nbank():
        b = bank_ctr[0] % NB
        bank_ctr[0] += 1
        return b

    es = ExitStack()
    sems = {}

    def sem(name):
        if name not in sems:
            sems[name] = es.enter_context(nc.semaphore(name))
        return sems[name]

    for e in COMPUTE:
        sem("e_" + e)

    def mm(out_, lhsT, rhs, start, stop, reads, writes, **kw):
        P.add(PE, lambda e: e.matmul(out_, lhsT, rhs, start=start, stop=stop, **kw), reads, writes)

    def tr(out_, in_, reads, writes):
        P.add(PE, lambda e: e.transpose(out_, in_, IDENT[:, :]), list(reads) + ["ident"], writes)

    def act(out_, in_, func, reads, writes, **kw):
        P.add(ACT, lambda e: e.activation(out_, in_, func, **kw), reads, writes)

    def cp(eng, out_, in_, reads, writes):
        if eng == ACT:
            P.add(ACT, lambda e: e.copy(out_, in_), reads, writes)
        else:
            P.add(eng, lambda e: e.tensor_copy(out_, in_), reads, writes)

    def tt(eng, out_, in0, in1, op, reads, writes):
        P.add(eng, lambda e: e.tensor_tensor(out_, in0, in1, op), reads, writes)

    def ts(eng, out_, in0, s1, s2, op0, op1, reads, writes):
        if op1 is None:
            P.add(eng, lambda e: e.tensor_scalar(out_, in0, s1, None, op0), reads, writes)
        else:
            P.add(eng, lambda e: e.tensor_scalar(out_, in0, s1, s2, op0, op1), reads, writes)

    def stt(eng, out_, in0, scalar, in1, op0, op1, reads, writes):
        P.add(eng, lambda e: e.scalar_tensor_tensor(out_, in0, scalar, in1, op0, op1), reads, writes)

    def dma(out_, in_, reads, writes, semname, eng=SP):
        P.add(eng, lambda e: e.dma_start(out_, in_), reads, writes, dma_sem=semname)
        sem(semname)

    def memset(eng, ap, val, writes):
        P.add(eng, lambda e: e.memset(ap, val), (), writes)

    dma(CST[:, :], cst32_d[:, :], [], ["cst"], "d_cst")
    dma(STG[0][:, 0:NCB], cstb_d[:, :], [], ["stg0"], "d_stg0")
    dma(HB[0][:, :, :], xmeta.rearrange("(jb p) d -> p jb d", p=128), [], [f"h0_{i}" for i in range(4)], "d_x0")
    cp(DVE, IDENT[:, :], STG[0][:, B_ID:B_ID + 128], ["stg0"], ["ident"])
    cp(DVE, BANDP[:, :, :], STG[0][:, B_BP:B_BP + 512].rearrange("p (g t) -> p g t", t=128), ["stg0"], ["bandp"])
    cp(DVE, BANDC[:, :, :], STG[0][:, B_BC:B_BC + 512].rearrange("p (g t) -> p g t", t=128), ["stg0"], ["bandc"])
    ts(DVE, NEGB[:, :], CST[:, C_BG:C_BG + 2], -1.0, None, ALU.mult, None, ["cst"], ["negb"])
    memset(DVE, SX[:, :, :, :, :], 0.0, [f"sx{i}{c}" for i in range(9) for c in "ab"])
    memset(DVE, KTE[:, :, :], 0.0, ["kte"])
    memset(DVE, KTO[:, :, :], 0.0, ["kto"])
    stg_i = [0]
    cast_i = [0]

    def prep_piece(src_ap, nk, ncol, scale_col, dst_sb=None, piece=None, k0=0):
        i = stg_i[0] % 2
        stg_i[0] += 1
        st = STG[i]
        stv = st[:, 0:nk * ncol].rearrange("p (k n) -> p k n", n=ncol)
        dma(stv, src_ap, [], [f"stg{i}"], f"d_stg{i}")
        if dst_sb is not None:
            dstv = dst_sb
            dkeys = ["wres"]
        else:
            slot = piece % NSLOTS
            dstv = RING[slot][:, 0:nk * ncol].rearrange("p (k n) -> p k n", n=ncol)
            dkeys = [f"ring{slot}"]
        if scale_col is None:
            half_n = nk // 2
            cp(DVE, dstv[:, 0:half_n, :], stv[:, 0:half_n, :], [f"stg{i}"], dkeys)
            cp(ACT, dstv[:, half_n:nk, :], stv[:, half_n:nk, :], [f"stg{i}"], dkeys)
        else:
            for k in range(nk):
                eng = (DVE, ACT)[cast_i[0] % 2]
                cast_i[0] += 1
                sc_ap = CST[:, scale_col + k0 + k:scale_col + k0 + k + 1]
                if eng == ACT:
                    act(dstv[:, k, :], stv[:, k, :], AF.Copy, [f"stg{i}", "cst"], dkeys, scale=sc_ap)
                else:
                    ts(eng, dstv[:, k, :], stv[:, k, :], sc_ap, None, ALU.mult, None, [f"stg{i}", "cst"], dkeys)
        if dst_sb is None:
            slot = piece % NSLOTS
            dma(wsc[piece, :, 0:nk * ncol], RING[slot][:, 0:nk * ncol], dkeys, [f"wsc{piece}"], f"d_ring{slot}", eng=ACT)

    prep_piece(w_lr_d[:, :, :], 8, 16, None, dst_sb=WLR[:, :, :])
    prep_piece(w_pm_d[:, :, :], 4, 128, None, dst_sb=WPM[:, :, :])
    i_ = stg_i[0] % 2
    stg_i[0] += 1
    dma(STG[i_][0:16, 0:256], w_gu_d[:, :], [], [f"stg{i_}"], f"d_stg{i_}")
    cp(DVE, WGU[:, :], STG[i_][0:16, 0:256], [f"stg{i_}"], ["wres"])

    CAST_LA = 8
    cast_next = [0]

    def emit_cast(extra_reads=()):
        pc = cast_next[0]
        if pc == 7:
            pc = 8
        if pc >= NPIECE:
            return
        cast_next[0] = pc + 1
        sem(f"d_sw{pc}")
        P.add(POOL, lambda e, pc=pc: e.dma_start(wsc[pc].rearrange("p (a n) -> p a n", n=1024),
                                                 wall_d[pc].rearrange("p (a n) -> p a n", n=1024)),
              list(extra_reads), [f"wsc{pc}"], dma_sem=f"d_sw{pc}")

    for _ in range(CAST_LA - 3):
        emit_cast()
    prep_piece(wall_d[7].rearrange("p (k n) -> p k n", n=1024), 4, 1024, C_GNP, piece=7)

    stream = {"next_load": 0, "seq": []}

    def plan_stream():
        seq = [0, 1, 2]
        for _ in range(n_seq * n_tiles):
            seq.extend(range(NPIECE))
        stream["seq"] = seq

    plan_stream()
    PIECE_LEN = {4: 4096, 5: 4096}

    def issue_load():
        n = stream["next_load"]
        if n >= len(stream["seq"]):
            return
        pc = stream["seq"][n]
        slot = n % NSLOTS
        ln = PIECE_LEN.get(pc, 4096)
        dma(RING[slot][:, 0:ln], wsc[pc, :, 0:ln], [f"wsc{pc}"], [f"ring{slot}", f"pace{n}"], f"d_ring{slot}")
        stream["next_load"] = n + 1
        emit_cast([f"pace{n}"])

    use_ctr = [0]

    def next_piece(expect):
        n = use_ctr[0]
        assert stream["seq"][n] == expect, (n, stream["seq"][n], expect)
        use_ctr[0] += 1
        slot = n % NSLOTS
        return RING[slot], f"ring{slot}"

    def done_piece():
        issue_load()

    P.alias(KEYS_STG, KEYS_GLA + KEYS_HDN)
    for _ in range(NSLOTS):
        issue_load()

    gc = [0]
    cur_s = [None]
    sctr = [0]
    UTK = [f"ut{k}" for k in range(8)]

    def hkeys(hb):
        return [f"h{hb}_{i}" for i in range(4)]

    def norm_a(hb, per_jb=False):
        Hh = HB[hb]
        for jb in range(4):
            act(UT[:, 2 * jb:2 * jb + 2, :].rearrange("p a t -> p (a t)"), Hh[:, jb, :], AF.Square,
                [f"h{hb}_{jb}"], [f"ut{2 * jb}", f"ut{2 * jb + 1}", f"ss{jb}"], accum_out=SS[:, jb:jb + 1])
            if per_jb:
                act(RS[:, jb:jb + 1], SS[:, jb:jb + 1], AF.Ln, [f"ss{jb}", "cst"], [f"rs{jb}"],
                    scale=1.0 / D, bias=CST[:, C_EPS:C_EPS + 1])
                act(RS[:, jb:jb + 1], RS[:, jb:jb + 1], AF.Exp, [f"rs{jb}"], [f"rs{jb}"], scale=-0.5)
                ts(DVE, HN[:, jb, :], Hh[:, jb, :], RS[:, jb:jb + 1], None, ALU.mult, None,
                   [f"h{hb}_{jb}", f"rs{jb}"], [f"hn{jb}"])
        if per_jb:
            return
        RSK = [f"rs{i}" for i in range(4)]
        act(RS[:, :], SS[:, :], AF.Ln, [f"ss{i}" for i in range(4)] + ["cst"], RSK,
            scale=1.0 / D, bias=CST[:, C_EPS:C_EPS + 1])
        act(RS[:, :], RS[:, :], AF.Exp, RSK, RSK, scale=-0.5)
        for jb in range(4):
            ts(DVE, HN[:, jb, :], Hh[:, jb, :], RS[:, jb:jb + 1], None, ALU.mult, None,
               [f"h{hb}_{jb}", f"rs{jb}"], [f"hn{jb}"])

    def norm_b(gcol, banks=None):
        for kc in range(8):
            b = nbank() if banks is None else banks[kc % len(banks)]
            pv = PS[b][:, :].bitcast(BF16)
            for jb in range(4):
                tr(pv[:, jb * 128:(jb + 1) * 128], HN[:, jb, kc * 128:(kc + 1) * 128],
                   [f"hn{jb}"], [f"ps{b}"])
            g_ap = CST[:, gcol + kc:gcol + kc + 1]
            if kc % 2:
                act(UT[:, kc, :], pv[:, 0:512], AF.Copy, [f"ps{b}", "cst"], [f"ut{kc}"], scale=g_ap)
            else:
                ts(DVE, UT[:, kc, :], pv[:, 0:512], g_ap, None, ALU.mult, None, [f"ps{b}", "cst"], [f"ut{kc}"])

    def load_x(kind, s, j, hb):
        src = xmeta if kind == "meta" else x[s, j * T:(j + 1) * T, :]
        dma(HB[hb][:, :, :], src.rearrange("(jb p) d -> p jb d", p=128), [], hkeys(hb), f"d_x{hb}")

    def proj_fm(w_ap, wkey, ncols_lo, evac):
        b = nbank()
        for kc in range(8):
            mm(PS[b][:, :], w_ap[:, kc, ncols_lo:ncols_lo + 128], UT[:, kc, :], kc == 0, kc == 7,
               [wkey, f"ut{kc}"], [f"ps{b}"])
        evac(b)

    def proj_tm(w_ap, wkey, jb, evac):
        b = nbank()
        for kc in range(8):
            mm(PS[b][:, :], UT[:, kc, jb * 128:(jb + 1) * 128], w_ap[:, kc, :], kc == 0, kc == 7,
               [wkey, f"ut{kc}"], [f"ps{b}"])
        evac(b)

    def emit_body(kind, s, j, hb, nxt):
        meta = kind == "meta"
        Hh = HB[hb]
        if meta:
            load_x(*nxt)
        P.alias(KEYS_HDN, KEYS_GLA)

        b = nbank()
        for kc in range(8):
            mm(PS[b][0:16, :], WLR[:, kc, :], UT[:, kc, :], kc == 0, kc == 7, ["wres", f"ut{kc}"], [f"ps{b}"])
        cp(ACT, LRT[:, :], PS[b][0:16, :], [f"ps{b}"], ["lrt"])
        for pr in range(2):
            b = nbank()
            mm(PS[b][:, :], WGU[:, pr * 128:(pr + 1) * 128], LRT[:, :], True, True, ["wres", "lrt"], [f"ps{b}"])
            act(E1L[:, pr, :], PS[b][:, :], AF.Exp, [f"ps{b}", "negb"], [f"e1l{pr}"], scale=-1.0, bias=NEGB[:, pr:pr + 1])
        W, wk = next_piece(0)
        Wv = W[:, :].rearrange("p (k n) -> p k n", n=512)
        for n in (2, 3, 0, 1):
            if meta and n < 2:
                continue
            proj_fm(Wv, wk, n * 128, lambda b, n=n: act(QK32[:, n, :], PS[b][:, :], AF.Copy, [f"ps{b}"], [f"qk{n}"],
                                                       scale=(0.125 if n < 2 else 1.0)))
        done_piece()
        for pr in range(2):
            act(E1L[:, pr, :], E1L[:, pr, :], AF.Ln, [f"e1l{pr}"], [f"e1l{pr}"], bias=1.0)
            P.add(DVE, lambda e, pr=pr: e.tensor_tensor_scan(CL[:, pr, :], CST[:, C_CM:C_CM + 512], E1L[:, pr, :],
                                                             0.0, ALU.mult, ALU.add),
                  [f"e1l{pr}", "cst"], [f"cl{pr}"])
            act(EP[:, pr, :], CL[:, pr, :], AF.Exp, [f"cl{pr}"], [f"ep{pr}"], scale=-1.0 / 16.0)
            act(EM[:, pr, :], CL[:, pr, :], AF.Exp, [f"cl{pr}"], [f"em{pr}"], scale=1.0 / 16.0)
        cp(DVE, DF[:, :, :], EP[:, :, :].rearrange("p a (c t) -> p a c t", t=64)[:, :, :, 63],
           ["ep0", "ep1"], ["df"])
        if pending_i[0] is not None:
            pending_i[0][0]()
        if meta:
            pass
        elif j == 0:
            cp(POOL, ATOK[:, 0, :], AMETA[:, :], ["ameta"], ["atok0"])
        else:
            cp(POOL, ATOK[:, 0, :], ATOK[:, 4, :], ["atok4"], ["atok0"])
        W, wk = next_piece(1)
        Wv = W[:, :].rearrange("p (k n) -> p k n", n=512)
        for jb in range(4):
            proj_tm(Wv, wk, jb, lambda b, jb=jb: cp(ACT, ATOK[:, jb + 1, :], PS[b][:, :],
                                                   [f"ps{b}"], [f"atok{jb + 1}"]))
        done_piece()
        if meta:
            cp(POOL, AMETA[:, :], ATOK[:, 4, :], ["atok4"], ["ameta"])
        for pr in range(2):
            tt(DVE, KM32[:, pr, :], QK32[:, 2 + pr, :], EM[:, pr, :], ALU.mult,
               [f"qk{2 + pr}", f"em{pr}"], [f"km32_{pr}"])
            tt(DVE, KT[:, pr, :].rearrange("p (c t) -> p c t", t=64),
               KM32[:, pr, :].rearrange("p (c t) -> p c t", t=64),
               DF[:, pr, :].unsqueeze(2).to_broadcast([128, 8, 64]), ALU.mult,
               [f"km32_{pr}", "df"], [f"kt{pr}"])
        if pending_i[0] is not None:
            pending_i[0][1]()
            pending_i[0] = None
        P.alias(KEYS_R32 + KEYS_G + KEYS_OUT, KEYS_DE)
        W, wk = next_piece(2)
        Wv = W[:, :].rearrange("p (k n) -> p k n", n=512)
        for jb in range(4):
            proj_tm(Wv, wk, jb, lambda b, jb=jb: cp(ACT, VTOK[:, jb, :], PS[b][:, :],
                                                   [f"ps{b}"], [f"vtok{jb}"]))
        done_piece()
        for pr in range(2):
            b = nbank()
            pv = PS[b][:, :].bitcast(BF16)
            for jb in range(4):
                tr(pv[:, jb * 128:(jb + 1) * 128], KT[:, pr, jb * 128:(jb + 1) * 128], [f"kt{pr}"], [f"ps{b}"])
            src_v = pv[:, 0:512].rearrange("p (jb n) -> p jb n", n=128)
            cp(DVE, KTE[0:64, :, pr * 128:(pr + 1) * 128], src_v[0:64, :, :], [f"ps{b}"], ["kte"])
            cp(ACT, KTO[64:128, :, pr * 128:(pr + 1) * 128], src_v[64:128, :, :], [f"ps{b}"], ["kto"])

        ubank = {}
        for cp2 in range(4):
            b = nbank()
            for cc in range(2):
                c = cp2 * 2 + cc
                ubank[c] = (b, cc)
                jb, cpar = c // 2, c % 2
                KX = KTE if cpar == 0 else KTO
                for h in range(4):
                    pr, par = h // 2, h % 2
                    o_ap = PS[b][64 * par:64 * par + 64, cc * 256 + pr * 128: cc * 256 + (pr + 1) * 128]
                    mm(o_ap, KX[:, jb, h * 64:(h + 1) * 64], VTOK[:, jb, h * 128:(h + 1) * 128], True, True,
                       ["kte" if cpar == 0 else "kto", f"vtok{jb}"], [f"ps{b}"])
        if meta:
            memset(DVE, SB2[2][:, :, :], 0.0, ["s2p0", "s2p1"])
            cur_s[0] = (SB2[2], "s2p")
        elif j == 0:
            cur_s[0] = (SMETA, "smetap")
            sl = gc[0] % 9
            cp(POOL, SX[0:64, sl, :, 0, :], SMETA[0:64, :, :], ["smetap0", "smetap1"], [f"sx{sl}a"])
            cp(ACT, SX[64:128, sl, :, 1, :], SMETA[64:128, :, :], ["smetap0", "smetap1"], [f"sx{sl}b"])
        slots = []
        for c in range(8):
            slots.append(gc[0] % 9)
            ub, ucc = ubank[c]
            S_old, ko = cur_s[0]
            nb_ = (sctr[0]) % 3
            sctr[0] += 1
            S_new, kn = SB2[nb_], f"s{nb_}p"
            for pr in range(2):
                stt(DVE, S_new[:, pr, :], S_old[:, pr, :], DF[:, pr, c:c + 1],
                    PS[ub][:, ucc * 256 + pr * 128:ucc * 256 + (pr + 1) * 128], ALU.mult, ALU.add,
                    [f"{ko}{pr}", "df", f"ps{ub}"], [f"{kn}{pr}"])
            cur_s[0] = (S_new, kn)
            if not meta:
                sl = (gc[0] + 1) % 9
                cp(POOL, SX[0:64, sl, :, 0, :], S_new[0:64, :, :], [f"{kn}0", f"{kn}1"], [f"sx{sl}a"])
                cp(ACT, SX[64:128, sl, :, 1, :], S_new[64:128, :, :], [f"{kn}0", f"{kn}1"], [f"sx{sl}b"])
                gc[0] += 1
        if not meta:
            for pr in range(2):
                act(KM[:, pr, :], KM32[:, pr, :], AF.Copy, [f"km32_{pr}"], [f"km{pr}"])
                tt(POOL, KP[:, pr, :], QK32[:, 2 + pr, :], EP[:, pr, :], ALU.mult,
                   [f"qk{2 + pr}", f"ep{pr}"], [f"kp{pr}"])
                tt(DVE, QP[:, pr, :], QK32[:, pr, :], EP[:, pr, :], ALU.mult,
                   [f"qk{pr}", f"ep{pr}"], [f"qp{pr}"])
                tt(DVE, QM[:, pr, :], QK32[:, pr, :], EM[:, pr, :], ALU.mult,
                   [f"qk{pr}", f"em{pr}"], [f"qm{pr}"])
        if meta:
            S_fin, kf = cur_s[0]
            cp(DVE, SMETA[:, :, :], S_fin[:, :, :], [f"{kf}0", f"{kf}1"], ["smetap0", "smetap1"])
            norm_a(nxt[3])
            norm_b(C_G1P)
            return

        for g in range(4):
            b = nbank()
            for jb in range(4):
                mm(PS[b][:, jb * 128:(jb + 1) * 128], ATOK[:, jb, g * 128:(g + 1) * 128], BANDP[:, g, :], True, False,
                   [f"atok{jb}", "bandp"], [f"ps{b}"])
                mm(PS[b][:, jb * 128:(jb + 1) * 128], ATOK[:, jb + 1, g * 128:(g + 1) * 128], BANDC[:, g, :], False, True,
                   [f"atok{jb + 1}", "bandc"], [f"ps{b}"])
            cp(ACT, PT[:, g, :], PS[b][:, :], [f"ps{b}"], [f"pt{g}"])

        W, wk = next_piece(3)
        Wv = W[:, :].rearrange("p (k n) -> p k n", n=512)
        for n in range(4):
            proj_fm(Wv, wk, n * 128, lambda b, n=n: act(SR[:, n, :], PS[b][:, :], AF.Silu, [f"ps{b}"], [f"sr{n}"]))
        done_piece()

        act(WARM[:, 0:1], CST[:, C_EPS:C_EPS + 1], AF.Ln, ["cst"], ["warm"])
        for g in range(4):
            b2 = nbank()
            mm(PS[b2][:, :], WPM[:, g, :], PT[:, g, :], True, True, ["wres", f"pt{g}"], [f"ps{b2}"])
            sc_ap = CST[:, C_PSC + g:C_PSC + g + 1]
            act(P2T[:, g, :], PS[b2][:, :], AF.Copy, [f"ps{b2}", "cst"], [f"p2t{g}"], scale=sc_ap)

        def e_scores(jb):
            bx = [nbank(), nbank()]
            for h in range(4):
                pr, par = h // 2, h % 2
                rows = slice(64 * par, 64 * par + 64)
                tok = slice(jb * 128, (jb + 1) * 128)
                mm(PS[bx[par]][:, pr * 128:(pr + 1) * 128], KM[rows, pr, tok], QP[rows, pr, tok], True, True,
                   [f"km{pr}", f"qp{pr}"], [f"ps{bx[par]}"])
                mm(PS[bx[par]][:, 256 + pr * 128:256 + (pr + 1) * 128], KP[rows, pr, tok], QM[rows, pr, tok], True, True,
                   [f"kp{pr}", f"qm{pr}"], [f"ps{bx[par]}"])
            for par in range(2):
                tt(DVE, T32B[:, jb, par, :], PS[bx[par]][:, :], CST[:, C_MCAT:C_MCAT + 512], ALU.mult,
                   [f"ps{bx[par]}", "cst"], [f"t32_{jb}_{par}"])

        obank = {}

        def e_out(jb):
            bo = nbank()
            obank[jb] = bo
            for h in range(4):
                pr, par = h // 2, h % 2
                for cc in range(2):
                    c = jb * 2 + cc
                    sl = slots[c]
                    o_ap = PS[bo][:, h * 128 + cc * 64:h * 128 + (cc + 1) * 64]
                    mm(o_ap, SX[:, sl, pr, par, :], QP[:, pr, c * 64:(c + 1) * 64], True, False,
                       [f"sx{sl}a", f"sx{sl}b", f"qp{pr}"], [f"ps{bo}"])
                    for ab in range(2):
                        c0 = ab * 256 + pr * 128 + cc * 64
                        mm(o_ap, VTOK[:, jb, h * 128:(h + 1) * 128], T32B[:, jb, par, c0:c0 + 64], False, ab == 1,
                           [f"vtok{jb}", f"t32_{jb}_{par}"], [f"ps{bo}"])
            act(OSQ[jb % 2][:, :], PS[bo][:, :], AF.Square, [f"ps{bo}"], [f"osq{jb % 2}"])

        def e_norm(jb):
            i2 = jb % 2
            bo = obank[jb]
            bm = nbank()
            mm(PS[bm][:, :], ONESB, OSQ[i2][:, :], True, True, ["cst", f"osq{i2}"], [f"ps{bm}"])
            act(RO[i2][:, :], PS[bm][:, :], AF.Ln, [f"ps{bm}", "cst"], [f"ro{i2}"], scale=1.0 / 128.0,
                bias=CST[:, C_EPS:C_EPS + 1])
            act(RO[i2][:, :], RO[i2][:, :], AF.Exp, [f"ro{i2}"], [f"ro{i2}"], scale=-0.5)
            tt(DVE, T1[:, :], PS[bo][:, :], RO[i2][:, :], ALU.mult, [f"ps{bo}", f"ro{i2}"], ["t1"])
            tt(POOL, OG[:, :, jb * 128:(jb + 1) * 128], T1[:, :].rearrange("p (h t) -> p h t", t=128),
               SR[:, :, jb * 128:(jb + 1) * 128], ALU.mult, ["t1"] + [f"sr{n}" for n in range(4)], [f"og{jb}"])

        e_scores(0)
        e_scores(1)
        e_out(0)
        e_scores(2)
        e_out(1)
        e_norm(0)
        e_scores(3)
        e_out(2)
        e_norm(1)
        e_out(3)
        e_norm(2)
        e_norm(3)

        P.alias(KEYS_DE, KEYS_G)
        P.alias(KEYS_HN, KEYS_MT)
        OGK = [f"og{i}" for i in range(4)]
        Wpo_v = Wgo_v = kpo = kgo = None
        for half in range(2):
            Wga, kga = next_piece(4 + 4 * half)
            Wgb, kgb = next_piece(5 + 4 * half)
            if half == 0:
                Wpo, kpo = next_piece(6)
                Wgo, kgo = next_piece(7)
                Wpo_v = Wpo[:, :].rearrange("p (g n) -> p g n", n=1024)
                Wgo_v = Wgo[:, :].rearrange("p (g n) -> p g n", n=1024)
            Wga_v = Wga[:, :].rearrange("p (k n) -> p k n", n=512)
            Wgb_v = Wgb[:, :].rearrange("p (k n) -> p k n", n=512)
            for dl in range(4):
                dch = half * 4 + dl
                i2 = dch % 2
                proj_fm(Wga_v, kga, dl * 128, lambda b: act(SGA[i2][:, :], PS[b][:, :], AF.Sigmoid,
                                                            [f"ps{b}"], [f"sga{i2}"]))
                proj_fm(Wgb_v, kgb, dl * 128, lambda b: act(SGB[i2][:, :], PS[b][:, :], AF.Sigmoid,
                                                            [f"ps{b}"], [f"sgb{i2}"]))
                ba = nbank()
                for g in range(4):
                    mm(PS[ba][:, :], Wpo_v[:, g, dch * 128:(dch + 1) * 128], P2T[:, g, :], g == 0, g == 3,
                       [kpo, f"p2t{g}"], [f"ps{ba}"])
                tt(DVE, TM1[i2][:, :], PS[ba][:, :], SGA[i2][:, :], ALU.mult, [f"ps{ba}", f"sga{i2}"], [f"tm1_{i2}"])
                bb = nbank()
                for g in range(4):
                    mm(PS[bb][:, :], Wgo_v[:, g, dch * 128:(dch + 1) * 128], OG[:, g, :], g == 0, g == 3,
                       [kgo] + OGK, [f"ps{bb}"])
                tt(DVE, TM2[i2][:, :], PS[bb][:, :], SGB[i2][:, :], ALU.mult, [f"ps{bb}", f"sgb{i2}"], [f"tm2_{i2}"])
                tt(DVE if dch == 7 else POOL, MT[:, dch, :], TM1[i2][:, :], TM2[i2][:, :], ALU.add, [f"tm1_{i2}", f"tm2_{i2}"], [f"mt{dch}"])
            for _ in range(2 if half == 0 else 4):
                done_piece()
        act(WARM[:, 0:1], CST[:, C_EPS:C_EPS + 1], AF.Ln, ["cst"], ["warm"])
        for half in range(2):
            W, wk = next_piece(10 + half)
            Wv = W[:, :].rearrange("p (k n) -> p k n", n=512)
            for jb in range(4):
                b = nbank()
                for dch in range(8):
                    mm(PS[b][:, :], MT[:, dch, jb * 128:(jb + 1) * 128], Wv[:, dch, :], dch == 0, dch == 7,
                       [wk, f"mt{dch}"], [f"ps{b}"])
                tt(DVE, Hh[:, jb, half * 512:(half + 1) * 512], PS[b][:, :], Hh[:, jb, half * 512:(half + 1) * 512],
                   ALU.add, [f"ps{b}", f"h{hb}_{jb}"], [f"h{hb}_{jb}"])
            done_piece()

        P.alias(KEYS_MT, KEYS_HN)
        P.alias(KEYS_GLA, KEYS_HDN)
        P.alias(KEYS_G, KEYS_R32 + KEYS_OUT)
        norm_a(hb, per_jb=True)
        norm_b(C_G2P)
        if nxt is not None:
            load_x(*nxt)
        for q in range(8):
            W, wk = next_piece(12 + q)
            Wv = W[:, :].rearrange("p (k n) -> p k n", n=512)
            for fl in range(4):
                fch = q * 4 + fl
                i4 = fch % 4
                b = nbank()
                for kc in range(8):
                    mm(PS[b][:, :], Wv[:, kc, fl * 128:(fl + 1) * 128], UT[:, kc, :], kc == 0, kc == 7,
                       [wk, f"ut{kc}"], [f"ps{b}"])
                act(R32[i4][:, :], PS[b][:, :], AF.Relu, [f"ps{b}"], [f"r32_{i4}"])
                tt(POOL if fch % 2 else DVE, HDN[:, fch, :], R32[i4][:, :], R32[i4][:, :], ALU.mult,
                   [f"r32_{i4}"], [f"hdn{fch}"])
            done_piece()
        for half in range(2):
            accb = [nbank() for _ in range(4)]
            for g4 in range(4):
                W, wk = next_piece(20 + half * 4 + g4)
                Wv = W[:, :].rearrange("p (k n) -> p k n", n=512)
                for fl in range(8):
                    fch = g4 * 8 + fl
                    for jb in range(4):
                        mm(PS[accb[jb]][:, :], HDN[:, fch, jb * 128:(jb + 1) * 128], Wv[:, fl, :],
                           fch == 0, fch == 31, [wk, f"hdn{fch}"], [f"ps{accb[jb]}"])
                done_piece()
            if half == 1 and nxt is not None:
                norm_a(nxt[3])
                norm_b(C_G1P, banks=[b for b in range(NB) if b not in accb])
            for jb in range(4):
                tt(DVE, Hh[:, jb, half * 512:(half + 1) * 512], PS[accb[jb]][:, :],
                   Hh[:, jb, half * 512:(half + 1) * 512], ALU.add, [f"ps{accb[jb]}", f"h{hb}_{jb}"], [f"h{hb}_{jb}"])

        def stage_i_act():
            for jb in range(4):
                act(JNK[:, jb % 2, :], Hh[:, jb, :], AF.Square, [f"h{hb}_{jb}"], [f"r32_{jb % 2}", f"ss{jb}"],
                    accum_out=SS[:, jb:jb + 1])
            act(RS[:, :], SS[:, :], AF.Ln, [f"ss{i}" for i in range(4)] + ["cst"], [f"rs{i}" for i in range(4)],
                scale=1.0 / D, bias=CST[:, C_EPS:C_EPS + 1])
            act(RS[:, :], RS[:, :], AF.Exp, [f"rs{i}" for i in range(4)], [f"rs{i}" for i in range(4)], scale=-0.5)

        def stage_i_out():
            for jb in range(4):
                ob = jb % 2
                stt(DVE, OUTB[:, ob, :], Hh[:, jb, :], RS[:, jb:jb + 1], CST[:, C_G3:C_G3 + D], ALU.mult, ALU.mult,
                    [f"h{hb}_{jb}", f"rs{jb}", "cst"], [f"outb{ob}"])
                dma(out[s, j * T + jb * 128:j * T + (jb + 1) * 128, :], OUTB[:, ob, :], [f"outb{ob}"],
                    [f"out_{s}_{j}_{jb}"], f"d_out{ob}")

        if nxt is None:
            stage_i_act()
            stage_i_out()
        else:
            pending_i[0] = (stage_i_act, stage_i_out)

    pending_i = [None]
    tiles = [("meta", 0, 0)] + [("main", s, j) for s in range(n_seq) for j in range(n_tiles)]
    norm_a(0)
    norm_b(C_G1P)
    for i, (kind, s, j) in enumerate(tiles):
        hb = i % 2
        nxt = None
        if i + 1 < len(tiles):
            nk, ns, nj = tiles[i + 1]
            nxt = (nk, ns, nj, (i + 1) % 2)
        emit_body(kind, s, j, hb, nxt)

    resolve(P)
    final_dma = [(sems[k], v) for k, v in P.dma_count.items() if k.startswith("d_out")]
    with es:
        with nc.Block() as block:
            def emit_engine(eng_name):
                def body(e):
                    for op in P.by_eng[eng_name]:
                        for w in op.waits:
                            if w[0] == "dma":
                                e.wait_ge(sems[w[1]], w[2])
                            else:
                                d = P.ops[w[2]]
                                e.wait_ge(sems["e_" + d.eng], d.rank)
                        ins = op.fn(e)
                        if op.dma_sem is not None:
                            ins.then_inc(sems[op.dma_sem], 16)
                        elif op.target:
                            ins.then_inc(sems["e_" + eng_name], 1)
                    if eng_name == SP:
                        for sm, v in final_dma:
                            e.wait_ge(sm, v)
                return body

            block.tensor(emit_engine(PE))
            block.scalar(emit_engine(ACT))
            block.vector(emit_engine(DVE))
            block.gpsimd(emit_engine(POOL))
            block.sync(emit_engine(SP))
    return nc


def resolve(P):
    for op in P.ops:
        E = op.eng
        lst = P.by_eng[E]
        prev = lst[op.pos - 1] if op.pos > 0 else None
        clock = dict(prev.clock) if prev is not None else {}
        waits = []
        for d_idx in sorted(op.deps):
            kind = op.deps[d_idx]
            d = P.ops[d_idx]
            if d.dma_sem is not None:
                key = ("dma", d.dma_sem)
                if clock.get(key, 0) >= d.dma_val:
                    continue
                waits.append(("dma", d.dma_sem, d.dma_val))
                clock[key] = d.dma_val
                continue
            if d.eng == SP:
                continue
            if d.eng == E and E == PE:
                continue
            if clock.get(d.eng, -1) >= d.pos:
                continue
            waits.append(("eng", d.eng, d_idx))
            d.target = True
            for k, v in d.clock.items():
                if clock.get(k, -1 if not isinstance(k, tuple) else 0) < v:
                    clock[k] = v
            clock[d.eng] = max(clock.get(d.eng, -1), d.pos)
        op.waits = waits
        op.clock = clock
    for e in COMPUTE:
        r = 0
        for op in P.by_eng[e]:
            if op.target:
                r += 1
                op.rank = r


def _host_consts(norm_mix, norm_ffn, norm_final, pool_scale, gla_norm, b_gate):
    c = np.zeros((128, NC32), np.float32)
    s_idx = np.arange(128)[:, None]
    t_idx = np.arange(128)[None, :]
    same = (s_idx // 64) == (t_idx // 64)
    m1 = (same & (s_idx <= t_idx)).astype(np.float32)
    m2 = (same & (s_idx > t_idx)).astype(np.float32)
    c[:, C_MCAT:C_MCAT + 512] = np.concatenate([m1, m1, m2, m2], axis=1)
    cm = np.ones(512, np.float32)
    cm[::64] = 0.0
    c[:, C_CM:C_CM + 512] = cm[None, :]
    c[:, C_ONES:C_ONES + 128] = np.array([0x3F803F80], np.uint32).view(np.float32)[0]
    c[:, C_G3:C_G3 + D] = norm_final[None, :]
    c[:, C_G1P:C_G1P + 8] = norm_mix.reshape(8, 128).T
    c[:, C_G2P:C_G2P + 8] = norm_ffn.reshape(8, 128).T
    c[:, C_PSC:C_PSC + 4] = pool_scale.reshape(4, 128).T
    c[:, C_GNP:C_GNP + 4] = gla_norm.reshape(4, 128).T
    c[:, C_BG:C_BG + 2] = b_gate.reshape(2, 128).T
    c[:, C_EPS] = EPS
    cb = np.zeros((128, NCB), np.float32)
    cb[:, B_ID:B_ID + 128] = np.eye(128, dtype=np.float32)
    for g, w in enumerate((2, 4, 8, 16)):
        dlt = t_idx - s_idx
        cur = ((dlt >= 0) & (dlt <= w - 1)).astype(np.float32) / w - (dlt == 0).astype(np.float32)
        dlp = t_idx + 128 - s_idx
        prv = ((dlp >= 0) & (dlp <= w - 1)).astype(np.float32) / w
        cb[:, B_BC + g * 128:B_BC + (g + 1) * 128] = cur
        cb[:, B_BP + g * 128:B_BP + (g + 1) * 128] = prv
    return c, cb


def _lay(w, nk):
    return np.ascontiguousarray(w.reshape(nk, 128, w.shape[-1]).transpose(1, 0, 2))


_NC_CACHE = {}


def run(x, meta_tokens, norm_mix, w_in, w_pool_mix, pool_scale, w_pool_out, w_gate_up, b_gate,
        gla_norm, w_gla_out, w_out, norm_ffn, w_ff1, w_ff2, norm_final, n_cores=N_CORES):
    x = np.asarray(x, np.float32)
    B, L, _ = x.shape
    n_seq = B // n_cores
    n_tiles = L // T
    key = (n_seq, n_tiles)
    if key not in _NC_CACHE:
        _NC_CACHE[key] = build(n_seq, n_tiles)
    nc = _NC_CACHE[key]
    f = lambda a: np.asarray(a, np.float32)
    c32, cb = _host_consts(f(norm_mix)[0], f(norm_ffn)[0], f(norm_final), f(pool_scale)[0], f(gla_norm)[0],
                           f(b_gate)[0])
    xmeta = np.zeros((T, D), np.float32)
    xmeta[T - 16:] = f(meta_tokens)
    w_in_l = _lay(f(w_in)[0], 8)
    w_ff1_l = _lay(f(w_ff1)[0], 8)
    w_ff2_l = _lay(f(w_ff2)[0], 32)
    w_out_l = _lay(f(w_out)[0], 8)
    wall = np.empty((28, 128, 4096), np.float32)
    in_cols = {0: 512, 1: 0, 2: 1024, 3: 1536, 4: 2064, 5: 3088, 8: 2576, 9: 3600}
    for pc in range(28):
        if pc in in_cols:
            c0 = in_cols[pc]
            blk = w_in_l[:, :, c0:c0 + 512]
        elif pc == 6:
            blk = _lay(f(w_pool_out)[0], 4)
        elif pc == 7:
            blk = _lay(f(w_gla_out)[0], 4)
        elif pc in (10, 11):
            hh = pc - 10
            blk = w_out_l[:, :, hh * 512:(hh + 1) * 512]
        elif 12 <= pc < 20:
            q = pc - 12
            blk = w_ff1_l[:, :, q * 512:(q + 1) * 512]
        else:
            q = pc - 20
            hh, g4 = q // 4, q % 4
            blk = w_ff2_l[:, g4 * 8:(g4 + 1) * 8, hh * 512:(hh + 1) * 512]
        wall[pc] = blk.reshape(128, 4096)
    shared = {
        "xmeta": xmeta, "cst32": c32, "cstb": cb, "wall": wall,
        "w_lr_l": np.ascontiguousarray(w_in_l[:, :, 2048:2064]),
        "w_pm_l": np.ascontiguousarray(f(w_pool_mix)[0].transpose(1, 0, 2)),
        "w_gu": np.ascontiguousarray(f(w_gate_up)[0]),
    }
    in_maps = []
    for c in range(n_cores):
        m = dict(shared)
        m["x"] = np.ascontiguousarray(x[c * n_seq:(c + 1) * n_seq])
        in_maps.append(m)
    res = run_bass_kernel_spmd(nc, in_maps, core_ids=list(range(n_cores)))
    return np.concatenate([np.asarray(r["out"], np.float32) for r in res.results], axis=0)


def kernel(**inputs):
    return run(**inputs)
```

```python
import numpy as np
from contextlib import ExitStack
import concourse.bass as bass
import concourse.mybir as mybir
from concourse.bass_utils import run_bass_kernel_spmd

F32 = mybir.dt.float32
BF16 = mybir.dt.bfloat16
AF = mybir.ActivationFunctionType
ALU = mybir.AluOpType

D = 1024
T = 512
NSLOTS = 5
EPS = 1e-6
N_CORES = 8

C_MCAT, C_CM, C_ONES, C_G3 = 0, 512, 1024, 1152
C_G1P, C_G2P, C_PSC, C_GNP, C_BG, C_EPS = 2176, 2184, 2192, 2196, 2200, 2202
NC32 = 2208
B_ID, B_BP, B_BC = 0, 128, 640
NCB = 1152

PE, ACT, DVE, POOL, SP = "pe", "act", "dve", "pool", "sp"
COMPUTE = (PE, ACT, DVE, POOL)


class Op:
    pass


class Prog:
    def __init__(self):
        self.ops = []
        self.by_eng = {e: [] for e in (PE, ACT, DVE, POOL, SP)}
        self.last_w = {}
        self.readers = {}
        self.dma_count = {}

    def add(self, eng, fn, reads=(), writes=(), dma_sem=None):
        op = Op()
        op.eng, op.fn, op.idx = eng, fn, len(self.ops)
        op.pos = len(self.by_eng[eng])
        op.dma_sem = dma_sem
        op.target = False
        op.rank = None
        if dma_sem is not None:
            self.dma_count[dma_sem] = self.dma_count.get(dma_sem, 0) + 16
            op.dma_val = self.dma_count[dma_sem]
        deps = {}
        for k in reads:
            w = self.last_w.get(k)
            if w is not None:
                deps[w] = "raw"
        for k in writes:
            w = self.last_w.get(k)
            if w is not None and w not in deps:
                deps[w] = "waw"
            for r in self.readers.get(k, ()):
                if r not in deps:
                    deps[r] = "war"
        deps.pop(op.idx, None)
        op.deps = deps
        for k in reads:
            self.readers.setdefault(k, []).append(op.idx)
        for k in writes:
            self.last_w[k] = op.idx
            self.readers[k] = []
        self.ops.append(op)
        self.by_eng[eng].append(op)
        return op

    def alias(self, old_keys, new_keys):
        acc = []
        for k in old_keys:
            w = self.last_w.get(k)
            if w is not None:
                acc.append(w)
            acc.extend(self.readers.get(k, ()))
        for k in new_keys:
            self.last_w.pop(k, None)
            self.readers[k] = list(acc)


def build(n_seq=4, n_tiles=4, debug=False):
    nc = bass.Bass("TRN2", target_bir_lowering=False)
    P = Prog()

    def din(name, shape):
        return nc.dram_tensor(name, list(shape), F32, kind="ExternalInput").ap()

    x = din("x", [n_seq, n_tiles * T, D])
    xmeta = din("xmeta", [T, D])
    cst32_d = din("cst32", [128, NC32])
    cstb_d = din("cstb", [128, NCB])
    wall_d = din("wall", [28, 128, 4096])
    w_lr_d = din("w_lr_l", [128, 8, 16])
    w_pm_d = din("w_pm_l", [128, 4, 128])
    w_gu_d = din("w_gu", [16, 256])
    out = nc.dram_tensor("out", [n_seq, n_tiles * T, D], F32, kind="ExternalOutput").ap()
    NPIECE = 28
    wsc = nc.dram_tensor("wsc", [NPIECE, 128, 4096], BF16, kind="Internal").ap()

    off = [16512]

    def sb(name, shape, dtype, at=None):
        esz = 4 if dtype == F32 else 2
        n = 1
        for s_ in shape[1:]:
            n *= s_
        nbytes = n * esz
        if at is None:
            at_ = off[0]
            off[0] += (nbytes + 63) // 64 * 64
        else:
            at_ = at
        t = nc.alloc_sbuf_tensor_at(name, list(shape), dtype, offset=at_)
        return t, at_, nbytes

    CST, _, _ = sb("cst", [128, NC32], F32)
    IDENT, _, _ = sb("ident", [128, 128], BF16)
    ONESB = CST[:, C_ONES:C_ONES + 64].bitcast(BF16)
    BANDP, _, _ = sb("bandp", [128, 4, 128], BF16)
    BANDC, _, _ = sb("bandc", [128, 4, 128], BF16)
    WLR, _, _ = sb("wlr", [128, 8, 16], BF16)
    WPM, _, _ = sb("wpm", [128, 4, 128], BF16)
    WGU, _, _ = sb("wgu", [16, 256], BF16)
    NEGB, _, _ = sb("negb", [128, 2], F32)
    H0, h_at, _ = sb("h0", [128, 4, D], F32)
    H1, _, _ = sb("h1", [128, 4, D], F32)
    HB = [H0, H1]
    HN, hn_at, _ = sb("hn", [128, 4, D], BF16)
    MT, _, _ = sb("mt", [128, 8, T], BF16, at=hn_at)
    UT, _, _ = sb("ut", [128, 8, T], BF16)
    ATOK, _, _ = sb("atok", [128, 5, 512], BF16)
    AMETA, _, _ = sb("ameta", [128, 512], BF16)
    VTOK, _, _ = sb("vtok", [128, 4, 512], BF16)
    SR, _, _ = sb("sr", [128, 4, T], F32)
    SS, ss_at, _ = sb("ss", [128, 4], F32)
    WARM = sb("warm", [128, 1], F32, at=ss_at + 32)[0]
    RS, _, _ = sb("rs", [128, 4], F32)
    HDN, r1, _ = sb("hdn", [128, 32, T], BF16)
    E1L, _, _ = sb("e1l", [128, 2, T], F32, at=r1)
    CL, _, _ = sb("cl", [128, 2, T], F32, at=r1 + 4096)
    EP, _, _ = sb("ep", [128, 2, T], F32, at=r1 + 8192)
    EM, _, _ = sb("em", [128, 2, T], F32, at=r1 + 12288)
    QK32, _, _ = sb("qk32", [128, 4, T], F32, at=r1 + 16384)
    KM32, _, _ = sb("km32", [128, 2, T], F32, at=r1 + 24576)
    KT, _, _ = sb("kt", [128, 2, T], BF16, at=r1 + 28672)
    LRT, _, _ = sb("lrt", [16, T], BF16, at=r1 + 30720)
    STG = [sb(f"stg{i}", [128, 4096], F32, at=r1 + 16384 * i)[0] for i in range(2)]
    STG.append(sb("stg2", [128, 4096], F32, at=h_at)[0])
    STG.append(sb("stg3", [128, 4096], F32, at=hn_at)[0])
    QP, _, _ = sb("qp", [128, 2, T], BF16)
    QM, _, _ = sb("qm", [128, 2, T], BF16)
    KP, _, _ = sb("kp", [128, 2, T], BF16)
    KM, _, _ = sb("km", [128, 2, T], BF16)
    KTE, _, _ = sb("kte", [128, 4, 256], BF16)
    KTO, _, _ = sb("kto", [128, 4, 256], BF16)
    DF, _, _ = sb("df", [128, 2, 8], F32)
    SB2 = [sb(f"s{i}", [128, 2, 128], F32)[0] for i in range(3)]
    SMETA, _, _ = sb("smeta", [128, 2, 128], F32)
    SX, _, _ = sb("sx", [128, 9, 2, 2, 128], BF16)
    OG, _, _ = sb("og", [128, 4, T], BF16)
    PT, _, _ = sb("pt", [128, 4, T], BF16)
    P2T, _, _ = sb("p2t", [128, 4, T], BF16)
    R2T, r2, _ = sb("r2t", [128, 4608], F32)
    T32B = sb("t32b", [128, 4, 2, 512], BF16, at=r2)[0]
    OSQ = [sb(f"osq_{i}", [128, 512], BF16, at=r2 + 8192 + 2048 * i)[0] for i in range(2)]
    RO = [sb(f"ro_{i}", [128, 512], F32, at=r2 + 12288 + 2048 * i)[0] for i in range(2)]
    T1 = sb("t1", [128, 512], F32, at=r2 + 16384)[0]
    SGA = [sb(f"sga_{i}", [128, 512], F32, at=r2 + 2048 * i)[0] for i in range(2)]
    SGB = [sb(f"sgb_{i}", [128, 512], F32, at=r2 + 4096 + 2048 * i)[0] for i in range(2)]
    TM1 = [sb(f"tm1_{i}", [128, 512], F32, at=r2 + 8192 + 2048 * i)[0] for i in range(2)]
    TM2 = [sb(f"tm2_{i}", [128, 512], F32, at=r2 + 12288 + 2048 * i)[0] for i in range(2)]
    R32 = [sb(f"r32_{i}", [128, 512], F32, at=r2 + 2048 * i)[0] for i in range(4)]
    OUTB = sb("outb", [128, 2, D], F32, at=r2 + 8192)[0]
    JNK = sb("jnk", [128, 2, D], BF16, at=r2)[0]
    RING = [sb(f"ring{i}", [128, 4096], BF16)[0] for i in range(NSLOTS)]
    assert off[0] <= 229376, off[0]

    KEYS_GLA = ["e1l0", "e1l1", "cl0", "cl1", "ep0", "ep1", "em0", "em1",
                "qk0", "qk1", "qk2", "qk3", "km32_0", "km32_1", "kt0", "kt1", "lrt"]
    KEYS_HDN = [f"hdn{i}" for i in range(32)]
    KEYS_STG = ["stg0", "stg1"]
    KEYS_DE = [f"t32_{i}_{p}" for i in range(4) for p in range(2)] + \
              [f"osq{i}" for i in range(2)] + [f"ro{i}" for i in range(2)] + ["t1"]
    KEYS_G = [f"sga{i}" for i in range(2)] + [f"sgb{i}" for i in range(2)] + \
             [f"tm1_{i}" for i in range(2)] + [f"tm2_{i}" for i in range(2)]
    KEYS_R32 = [f"r32_{i}" for i in range(4)]
    KEYS_HN = [f"hn{i}" for i in range(4)]
    KEYS_MT = [f"mt{i}" for i in range(8)]
    KEYS_OUT = ["outb0", "outb1"]

    NB = 8
    PS = [nc.alloc_psum_tensor(f"ps{i}", [128, 512], F32) for i in range(NB)]
    bank_ctr = [0]

    def nbank():
        b = bank_ctr[0] % NB
        bank_ctr[0] += 1
        return b

    es = ExitStack()
    sems = {}

    def sem(name):
        if name not in sems:
            sems[name] = es.enter_context(nc.semaphore(name))
        return sems[name]

    for e in COMPUTE:
        sem("e_" + e)

    def mm(out_, lhsT, rhs, start, stop, reads, writes, **kw):
        P.add(PE, lambda e: e.matmul(out_, lhsT, rhs, start=start, stop=stop, **kw), reads, writes)

    def tr(out_, in_, reads, writes):
        P.add(PE, lambda e: e.transpose(out_, in_, IDENT[:, :]), list(reads) + ["ident"], writes)

    def act(out_, in_, func, reads, writes, **kw):
        P.add(ACT, lambda e: e.activation(out_, in_, func, **kw), reads, writes)

    def cp(eng, out_, in_, reads, writes):
        if eng == ACT:
            P.add(ACT, lambda e: e.copy(out_, in_), reads, writes)
        else:
            P.add(eng, lambda e: e.tensor_copy(out_, in_), reads, writes)

    def tt(eng, out_, in0, in1, op, reads, writes):
        P.add(eng, lambda e: e.tensor_tensor(out_, in0, in1, op), reads, writes)

    def ts(eng, out_, in0, s1, s2, op0, op1, reads, writes):
        if op1 is None:
            P.add(eng, lambda e: e.tensor_scalar(out_, in0, s1, None, op0), reads, writes)
        else:
            P.add(eng, lambda e: e.tensor_scalar(out_, in0, s1, s2, op0, op1), reads, writes)

    def stt(eng, out_, in0, scalar, in1, op0, op1, reads, writes):
        P.add(eng, lambda e: e.scalar_tensor_tensor(out_, in0, scalar, in1, op0, op1), reads, writes)

    def dma(out_, in_, reads, writes, semname, eng=SP):
        P.add(eng, lambda e: e.dma_start(out_, in_), reads, writes, dma_sem=semname)
        sem(semname)

    def memset(eng, ap, val, writes):
        P.add(eng, lambda e: e.memset(ap, val), (), writes)

    dma(CST[:, :], cst32_d[:, :], [], ["cst"], "d_cst")
    dma(STG[0][:, 0:NCB], cstb_d[:, :], [], ["stg0"], "d_stg0")
    dma(HB[0][:, :, :], xmeta.rearrange("(jb p) d -> p jb d", p=128), [], [f"h0_{i}" for i in range(4)], "d_x0")
    cp(DVE, IDENT[:, :], STG[0][:, B_ID:B_ID + 128], ["stg0"], ["ident"])
    cp(DVE, BANDP[:, :, :], STG[0][:, B_BP:B_BP + 512].rearrange("p (g t) -> p g t", t=128), ["stg0"], ["bandp"])
    cp(DVE, BANDC[:, :, :], STG[0][:, B_BC:B_BC + 512].rearrange("p (g t) -> p g t", t=128), ["stg0"], ["bandc"])
    ts(DVE, NEGB[:, :], CST[:, C_BG:C_BG + 2], -1.0, None, ALU.mult, None, ["cst"], ["negb"])
    memset(DVE, SX[:, :, :, :, :], 0.0, [f"sx{i}{c}" for i in range(9) for c in "ab"])
    memset(DVE, KTE[:, :, :], 0.0, ["kte"])
    memset(DVE, KTO[:, :, :], 0.0, ["kto"])
    stg_i = [0]
    cast_i = [0]

    def prep_piece(src_ap, nk, ncol, scale_col, dst_sb=None, piece=None, k0=0):
        i = stg_i[0] % 2
        stg_i[0] += 1
        st = STG[i]
        stv = st[:, 0:nk * ncol].rearrange("p (k n) -> p k n", n=ncol)
        dma(stv, src_ap, [], [f"stg{i}"], f"d_stg{i}")
        if dst_sb is not None:
            dstv = dst_sb
            dkeys = ["wres"]
        else:
            slot = piece % NSLOTS
            dstv = RING[slot][:, 0:nk * ncol].rearrange("p (k n) -> p k n", n=ncol)
            dkeys = [f"ring{slot}"]
        if scale_col is None:
            half_n = nk // 2
            cp(DVE, dstv[:, 0:half_n, :], stv[:, 0:half_n, :], [f"stg{i}"], dkeys)
            cp(ACT, dstv[:, half_n:nk, :], stv[:, half_n:nk, :], [f"stg{i}"], dkeys)
        else:
            for k in range(nk):
                eng = (DVE, ACT)[cast_i[0] % 2]
                cast_i[0] += 1
                sc_ap = CST[:, scale_col + k0 + k:scale_col + k0 + k + 1]
                if eng == ACT:
                    act(dstv[:, k, :], stv[:, k, :], AF.Copy, [f"stg{i}", "cst"], dkeys, scale=sc_ap)
                else:
                    ts(eng, dstv[:, k, :], stv[:, k, :], sc_ap, None, ALU.mult, None, [f"stg{i}", "cst"], dkeys)
        if dst_sb is None:
            slot = piece % NSLOTS
            dma(wsc[piece, :, 0:nk * ncol], RING[slot][:, 0:nk * ncol], dkeys, [f"wsc{piece}"], f"d_ring{slot}", eng=ACT)

    prep_piece(w_lr_d[:, :, :], 8, 16, None, dst_sb=WLR[:, :, :])
    prep_piece(w_pm_d[:, :, :], 4, 128, None, dst_sb=WPM[:, :, :])
    i_ = stg_i[0] % 2
    stg_i[0] += 1
    dma(STG[i_][0:16, 0:256], w_gu_d[:, :], [], [f"stg{i_}"], f"d_stg{i_}")
    cp(DVE, WGU[:, :], STG[i_][0:16, 0:256], [f"stg{i_}"], ["wres"])

    CAST_LA = 8
    cast_next = [0]

    def emit_cast(extra_reads=()):
        pc = cast_next[0]
        if pc == 7:
            pc = 8
        if pc >= NPIECE:
            return
        cast_next[0] = pc + 1
        sem(f"d_sw{pc}")
        P.add(POOL, lambda e, pc=pc: e.dma_start(wsc[pc].rearrange("p (a n) -> p a n", n=1024),
                                                 wall_d[pc].rearrange("p (a n) -> p a n", n=1024)),
              list(extra_reads), [f"wsc{pc}"], dma_sem=f"d_sw{pc}")

    for _ in range(CAST_LA - 3):
        emit_cast()
    prep_piece(wall_d[7].rearrange("p (k n) -> p k n", n=1024), 4, 1024, C_GNP, piece=7)

    stream = {"next_load": 0, "seq": []}

    def plan_stream():
        seq = [0, 1, 2]
        for _ in range(n_seq * n_tiles):
            seq.extend(range(NPIECE))
        stream["seq"] = seq

    plan_stream()
    PIECE_LEN = {4: 4096, 5: 4096}

    def issue_load():
        n = stream["next_load"]
        if n >= len(stream["seq"]):
            return
        pc = stream["seq"][n]
        slot = n % NSLOTS
        ln = PIECE_LEN.get(pc, 4096)
        dma(RING[slot][:, 0:ln], wsc[pc, :, 0:ln], [f"wsc{pc}"], [f"ring{slot}", f"pace{n}"], f"d_ring{slot}")
        stream["next_load"] = n + 1
        emit_cast([f"pace{n}"])

    use_ctr = [0]

    def next_piece(expect):
        n = use_ctr[0]
        assert stream["seq"][n] == expect, (n, stream["seq"][n], expect)
        use_ctr[0] += 1
        slot = n % NSLOTS
        return RING[slot], f"ring{slot}"

    def done_piece():
        issue_load()

    P.alias(KEYS_STG, KEYS_GLA + KEYS_HDN)
    for _ in range(NSLOTS):
        issue_load()

    gc = [0]
    cur_s = [None]
    sctr = [0]
    UTK = [f"ut{k}" for k in range(8)]

    def hkeys(hb):
        return [f"h{hb}_{i}" for i in range(4)]

    def norm_a(hb, per_jb=False):
        Hh = HB[hb]
        for jb in range(4):
            act(UT[:, 2 * jb:2 * jb + 2, :].rearrange("p a t -> p (a t)"), Hh[:, jb, :], AF.Square,
                [f"h{hb}_{jb}"], [f"ut{2 * jb}", f"ut{2 * jb + 1}", f"ss{jb}"], accum_out=SS[:, jb:jb + 1])
            if per_jb:
                act(RS[:, jb:jb + 1], SS[:, jb:jb + 1], AF.Ln, [f"ss{jb}", "cst"], [f"rs{jb}"],
                    scale=1.0 / D, bias=CST[:, C_EPS:C_EPS + 1])
                act(RS[:, jb:jb + 1], RS[:, jb:jb + 1], AF.Exp, [f"rs{jb}"], [f"rs{jb}"], scale=-0.5)
                ts(DVE, HN[:, jb, :], Hh[:, jb, :], RS[:, jb:jb + 1], None, ALU.mult, None,
                   [f"h{hb}_{jb}", f"rs{jb}"], [f"hn{jb}"])
        if per_jb:
            return
        RSK = [f"rs{i}" for i in range(4)]
        act(RS[:, :], SS[:, :], AF.Ln, [f"ss{i}" for i in range(4)] + ["cst"], RSK,
            scale=1.0 / D, bias=CST[:, C_EPS:C_EPS + 1])
        act(RS[:, :], RS[:, :], AF.Exp, RSK, RSK, scale=-0.5)
        for jb in range(4):
            ts(DVE, HN[:, jb, :], Hh[:, jb, :], RS[:, jb:jb + 1], None, ALU.mult, None,
               [f"h{hb}_{jb}", f"rs{jb}"], [f"hn{jb}"])

    def norm_b(gcol, banks=None):
        for kc in range(8):
            b = nbank() if banks is None else banks[kc % len(banks)]
            pv = PS[b][:, :].bitcast(BF16)
            for jb in range(4):
                tr(pv[:, jb * 128:(jb + 1) * 128], HN[:, jb, kc * 128:(kc + 1) * 128],
                   [f"hn{jb}"], [f"ps{b}"])
            g_ap = CST[:, gcol + kc:gcol + kc + 1]
            if kc % 2:
                act(UT[:, kc, :], pv[:, 0:512], AF.Copy, [f"ps{b}", "cst"], [f"ut{kc}"], scale=g_ap)
            else:
                ts(DVE, UT[:, kc, :], pv[:, 0:512], g_ap, None, ALU.mult, None, [f"ps{b}", "cst"], [f"ut{kc}"])

    def load_x(kind, s, j, hb):
        src = xmeta if kind == "meta" else x[s, j * T:(j + 1) * T, :]
        dma(HB[hb][:, :, :], src.rearrange("(jb p) d -> p jb d", p=128), [], hkeys(hb), f"d_x{hb}")

    def proj_fm(w_ap, wkey, ncols_lo, evac):
        b = nbank()
        for kc in range(8):
            mm(PS[b][:, :], w_ap[:, kc, ncols_lo:ncols_lo + 128], UT[:, kc, :], kc == 0, kc == 7,
               [wkey, f"ut{kc}"], [f"ps{b}"])
        evac(b)

    def proj_tm(w_ap, wkey, jb, evac):
        b = nbank()
        for kc in range(8):
            mm(PS[b][:, :], UT[:, kc, jb * 128:(jb + 1) * 128], w_ap[:, kc, :], kc == 0, kc == 7,
               [wkey, f"ut{kc}"], [f"ps{b}"])
        evac(b)

    def emit_body(kind, s, j, hb, nxt):
        meta = kind == "meta"
        Hh = HB[hb]
        if meta:
            load_x(*nxt)
        P.alias(KEYS_HDN, KEYS_GLA)

        b = nbank()
        for kc in range(8):
            mm(PS[b][0:16, :], WLR[:, kc, :], UT[:, kc, :], kc == 0, kc == 7, ["wres", f"ut{kc}"], [f"ps{b}"])
        cp(ACT, LRT[:, :], PS[b][0:16, :], [f"ps{b}"], ["lrt"])
        for pr in range(2):
            b = nbank()
            mm(PS[b][:, :], WGU[:, pr * 128:(pr + 1) * 128], LRT[:, :], True, True, ["wres", "lrt"], [f"ps{b}"])
            act(E1L[:, pr, :], PS[b][:, :], AF.Exp, [f"ps{b}", "negb"], [f"e1l{pr}"], scale=-1.0, bias=NEGB[:, pr:pr + 1])
        W, wk = next_piece(0)
        Wv = W[:, :].rearrange("p (k n) -> p k n", n=512)
        for n in (2, 3, 0, 1):
            if meta and n < 2:
                continue
            proj_fm(Wv, wk, n * 128, lambda b, n=n: act(QK32[:, n, :], PS[b][:, :], AF.Copy, [f"ps{b}"], [f"qk{n}"],
                                                       scale=(0.125 if n < 2 else 1.0)))
        done_piece()
        for pr in range(2):
            act(E1L[:, pr, :], E1L[:, pr, :], AF.Ln, [f"e1l{pr}"], [f"e1l{pr}"], bias=1.0)
            P.add(DVE, lambda e, pr=pr: e.tensor_tensor_scan(CL[:, pr, :], CST[:, C_CM:C_CM + 512], E1L[:, pr, :],
                                                             0.0, ALU.mult, ALU.add),
                  [f"e1l{pr}", "cst"], [f"cl{pr}"])
            act(EP[:, pr, :], CL[:, pr, :], AF.Exp, [f"cl{pr}"], [f"ep{pr}"], scale=-1.0 / 16.0)
            act(EM[:, pr, :], CL[:, pr, :], AF.Exp, [f"cl{pr}"], [f"em{pr}"], scale=1.0 / 16.0)
        cp(DVE, DF[:, :, :], EP[:, :, :].rearrange("p a (c t) -> p a c t", t=64)[:, :, :, 63],
           ["ep0", "ep1"], ["df"])
        if pending_i[0] is not None:
            pending_i[0][0]()
        if meta:
            pass
        elif j == 0:
            cp(POOL, ATOK[:, 0, :], AMETA[:, :], ["ameta"], ["atok0"])
        else:
            cp(POOL, ATOK[:, 0, :], ATOK[:, 4, :], ["atok4"], ["atok0"])
        W, wk = next_piece(1)
        Wv = W[:, :].rearrange("p (k n) -> p k n", n=512)
        for jb in range(4):
            proj_tm(Wv, wk, jb, lambda b, jb=jb: cp(ACT, ATOK[:, jb + 1, :], PS[b][:, :],
                                                   [f"ps{b}"], [f"atok{jb + 1}"]))
        done_piece()
        if meta:
            cp(POOL, AMETA[:, :], ATOK[:, 4, :], ["atok4"], ["ameta"])
        for pr in range(2):
            tt(DVE, KM32[:, pr, :], QK32[:, 2 + pr, :], EM[:, pr, :], ALU.mult,
               [f"qk{2 + pr}", f"em{pr}"], [f"km32_{pr}"])
            tt(DVE, KT[:, pr, :].rearrange("p (c t) -> p c t", t=64),
               KM32[:, pr, :].rearrange("p (c t) -> p c t", t=64),
               DF[:, pr, :].unsqueeze(2).to_broadcast([128, 8, 64]), ALU.mult,
               [f"km32_{pr}", "df"], [f"kt{pr}"])
        if pending_i[0] is not None:
            pending_i[0][1]()
            pending_i[0] = None
        P.alias(KEYS_R32 + KEYS_G + KEYS_OUT, KEYS_DE)
        W, wk = next_piece(2)
        Wv = W[:, :].rearrange("p (k n) -> p k n", n=512)
        for jb in range(4):
            proj_tm(Wv, wk, jb, lambda b, jb=jb: cp(ACT, VTOK[:, jb, :], PS[b][:, :],
                                                   [f"ps{b}"], [f"vtok{jb}"]))
        done_piece()
        for pr in range(2):
            b = nbank()
            pv = PS[b][:, :].bitcast(BF16)
            for jb in range(4):
                tr(pv[:, jb * 128:(jb + 1) * 128], KT[:, pr, jb * 128:(jb + 1) * 128], [f"kt{pr}"], [f"ps{b}"])
            src_v = pv[:, 0:512].rearrange("p (jb n) -> p jb n", n=128)
            cp(DVE, KTE[0:64, :, pr * 128:(pr + 1) * 128], src_v[0:64, :, :], [f"ps{b}"], ["kte"])
            cp(ACT, KTO[64:128, :, pr * 128:(pr + 1) * 128], src_v[64:128, :, :], [f"ps{b}"], ["kto"])

        ubank = {}
        for cp2 in range(4):
            b = nbank()
            for cc in range(2):
                c = cp2 * 2 + cc
                ubank[c] = (b, cc)
                jb, cpar = c // 2, c % 2
                KX = KTE if cpar == 0 else KTO
                for h in range(4):
                    pr, par = h // 2, h % 2
                    o_ap = PS[b][64 * par:64 * par + 64, cc * 256 + pr * 128: cc * 256 + (pr + 1) * 128]
                    mm(o_ap, KX[:, jb, h * 64:(h + 1) * 64], VTOK[:, jb, h * 128:(h + 1) * 128], True, True,
                       ["kte" if cpar == 0 else "kto", f"vtok{jb}"], [f"ps{b}"])
        if meta:
            memset(DVE, SB2[2][:, :, :], 0.0, ["s2p0", "s2p1"])
            cur_s[0] = (SB2[2], "s2p")
        elif j == 0:
            cur_s[0] = (SMETA, "smetap")
            sl = gc[0] % 9
            cp(POOL, SX[0:64, sl, :, 0, :], SMETA[0:64, :, :], ["smetap0", "smetap1"], [f"sx{sl}a"])
            cp(ACT, SX[64:128, sl, :, 1, :], SMETA[64:128, :, :], ["smetap0", "smetap1"], [f"sx{sl}b"])
        slots = []
        for c in range(8):
            slots.append(gc[0] % 9)
            ub, ucc = ubank[c]
            S_old, ko = cur_s[0]
            nb_ = (sctr[0]) % 3
            sctr[0] += 1
            S_new, kn = SB2[nb_], f"s{nb_}p"
            for pr in range(2):
                stt(DVE, S_new[:, pr, :], S_old[:, pr, :], DF[:, pr, c:c + 1],
                    PS[ub][:, ucc * 256 + pr * 128:ucc * 256 + (pr + 1) * 128], ALU.mult, ALU.add,
                    [f"{ko}{pr}", "df", f"ps{ub}"], [f"{kn}{pr}"])
            cur_s[0] = (S_new, kn)
            if not meta:
                sl = (gc[0] + 1) % 9
                cp(POOL if c % 2 == 0 else DVE, SX[0:64, sl, :, 0, :], S_new[0:64, :, :], [f"{kn}0", f"{kn}1"], [f"sx{sl}a"])
                cp(ACT, SX[64:128, sl, :, 1, :], S_new[64:128, :, :], [f"{kn}0", f"{kn}1"], [f"sx{sl}b"])
                gc[0] += 1
        if not meta:
            for pr in range(2):
                act(KM[:, pr, :], KM32[:, pr, :], AF.Copy, [f"km32_{pr}"], [f"km{pr}"])
                tt(POOL, KP[:, pr, :], QK32[:, 2 + pr, :], EP[:, pr, :], ALU.mult,
                   [f"qk{2 + pr}", f"ep{pr}"], [f"kp{pr}"])
                tt(DVE, QP[:, pr, :], QK32[:, pr, :], EP[:, pr, :], ALU.mult,
                   [f"qk{pr}", f"ep{pr}"], [f"qp{pr}"])
                tt(DVE, QM[:, pr, :], QK32[:, pr, :], EM[:, pr, :], ALU.mult,
                   [f"qk{pr}", f"em{pr}"], [f"qm{pr}"])
        if meta:
            S_fin, kf = cur_s[0]
            cp(DVE, SMETA[:, :, :], S_fin[:, :, :], [f"{kf}0", f"{kf}1"], ["smetap0", "smetap1"])
            norm_a(nxt[3])
            norm_b(C_G1P)
            return

        for g in range(4):
            b = nbank()
            for jb in range(4):
                mm(PS[b][:, jb * 128:(jb + 1) * 128], ATOK[:, jb, g * 128:(g + 1) * 128], BANDP[:, g, :], True, False,
                   [f"atok{jb}", "bandp"], [f"ps{b}"])
                mm(PS[b][:, jb * 128:(jb + 1) * 128], ATOK[:, jb + 1, g * 128:(g + 1) * 128], BANDC[:, g, :], False, True,
                   [f"atok{jb + 1}", "bandc"], [f"ps{b}"])
            cp(ACT, PT[:, g, :], PS[b][:, :], [f"ps{b}"], [f"pt{g}"])

        W, wk = next_piece(3)
        Wv = W[:, :].rearrange("p (k n) -> p k n", n=512)
        for n in range(4):
            proj_fm(Wv, wk, n * 128, lambda b, n=n: act(SR[:, n, :], PS[b][:, :], AF.Silu, [f"ps{b}"], [f"sr{n}"]))
        done_piece()

        act(WARM[:, 0:1], CST[:, C_EPS:C_EPS + 1], AF.Ln, ["cst"], ["warm"])
        for g in range(4):
            b2 = nbank()
            mm(PS[b2][:, :], WPM[:, g, :], PT[:, g, :], True, True, ["wres", f"pt{g}"], [f"ps{b2}"])
            sc_ap = CST[:, C_PSC + g:C_PSC + g + 1]
            act(P2T[:, g, :], PS[b2][:, :], AF.Copy, [f"ps{b2}", "cst"], [f"p2t{g}"], scale=sc_ap)

        def e_scores(jb):
            bx = [nbank(), nbank()]
            for h in range(4):
                pr, par = h // 2, h % 2
                rows = slice(64 * par, 64 * par + 64)
                tok = slice(jb * 128, (jb + 1) * 128)
                mm(PS[bx[par]][:, pr * 128:(pr + 1) * 128], KM[rows, pr, tok], QP[rows, pr, tok], True, True,
                   [f"km{pr}", f"qp{pr}"], [f"ps{bx[par]}"])
                mm(PS[bx[par]][:, 256 + pr * 128:256 + (pr + 1) * 128], KP[rows, pr, tok], QM[rows, pr, tok], True, True,
                   [f"kp{pr}", f"qm{pr}"], [f"ps{bx[par]}"])
            for par in range(2):
                tt(DVE, T32B[:, jb, par, :], PS[bx[par]][:, :], CST[:, C_MCAT:C_MCAT + 512], ALU.mult,
                   [f"ps{bx[par]}", "cst"], [f"t32_{jb}_{par}"])

        obank = {}

        def e_out(jb):
            bo = nbank()
            obank[jb] = bo
            for h in range(4):
                pr, par = h // 2, h % 2
                for cc in range(2):
                    c = jb * 2 + cc
                    sl = slots[c]
                    o_ap = PS[bo][:, h * 128 + cc * 64:h * 128 + (cc + 1) * 64]
                    mm(o_ap, SX[:, sl, pr, par, :], QP[:, pr, c * 64:(c + 1) * 64], True, False,
                       [f"sx{sl}a", f"sx{sl}b", f"qp{pr}"], [f"ps{bo}"])
                    for ab in range(2):
                        c0 = ab * 256 + pr * 128 + cc * 64
                        mm(o_ap, VTOK[:, jb, h * 128:(h + 1) * 128], T32B[:, jb, par, c0:c0 + 64], False, ab == 1,
                           [f"vtok{jb}", f"t32_{jb}_{par}"], [f"ps{bo}"])
            act(OSQ[jb % 2][:, :], PS[bo][:, :], AF.Square, [f"ps{bo}"], [f"osq{jb % 2}"])

        def e_norm(jb):
            i2 = jb % 2
            bo = obank[jb]
            bm = nbank()
            mm(PS[bm][:, :], ONESB, OSQ[i2][:, :], True, True, ["cst", f"osq{i2}"], [f"ps{bm}"])
            act(RO[i2][:, :], PS[bm][:, :], AF.Ln, [f"ps{bm}", "cst"], [f"ro{i2}"], scale=1.0 / 128.0,
                bias=CST[:, C_EPS:C_EPS + 1])
            act(RO[i2][:, :], RO[i2][:, :], AF.Exp, [f"ro{i2}"], [f"ro{i2}"], scale=-0.5)
            tt(DVE, T1[:, :], PS[bo][:, :], RO[i2][:, :], ALU.mult, [f"ps{bo}", f"ro{i2}"], ["t1"])
            tt(POOL, OG[:, :, jb * 128:(jb + 1) * 128], T1[:, :].rearrange("p (h t) -> p h t", t=128),
               SR[:, :, jb * 128:(jb + 1) * 128], ALU.mult, ["t1"] + [f"sr{n}" for n in range(4)], [f"og{jb}"])

        e_scores(0)
        e_scores(1)
        e_out(0)
        e_scores(2)
        e_out(1)
        e_norm(0)
        e_scores(3)
        e_out(2)
        e_norm(1)
        e_out(3)
        e_norm(2)
        e_norm(3)

        P.alias(KEYS_DE, KEYS_G)
        P.alias(KEYS_HN, KEYS_MT)
        OGK = [f"og{i}" for i in range(4)]
        Wpo_v = Wgo_v = kpo = kgo = None
        for half in range(2):
            Wga, kga = next_piece(4 + 4 * half)
            Wgb, kgb = next_piece(5 + 4 * half)
            if half == 0:
                Wpo, kpo = next_piece(6)
                Wgo, kgo = next_piece(7)
                Wpo_v = Wpo[:, :].rearrange("p (g n) -> p g n", n=1024)
                Wgo_v = Wgo[:, :].rearrange("p (g n) -> p g n", n=1024)
            Wga_v = Wga[:, :].rearrange("p (k n) -> p k n", n=512)
            Wgb_v = Wgb[:, :].rearrange("p (k n) -> p k n", n=512)
            for dl in range(4):
                dch = half * 4 + dl
                i2 = dch % 2
                proj_fm(Wga_v, kga, dl * 128, lambda b: act(SGA[i2][:, :], PS[b][:, :], AF.Sigmoid,
                                                            [f"ps{b}"], [f"sga{i2}"]))
                proj_fm(Wgb_v, kgb, dl * 128, lambda b: act(SGB[i2][:, :], PS[b][:, :], AF.Sigmoid,
                                                            [f"ps{b}"], [f"sgb{i2}"]))
                ba = nbank()
                for g in range(4):
                    mm(PS[ba][:, :], Wpo_v[:, g, dch * 128:(dch + 1) * 128], P2T[:, g, :], g == 0, g == 3,
                       [kpo, f"p2t{g}"], [f"ps{ba}"])
                tt(DVE, TM1[i2][:, :], PS[ba][:, :], SGA[i2][:, :], ALU.mult, [f"ps{ba}", f"sga{i2}"], [f"tm1_{i2}"])
                bb = nbank()
                for g in range(4):
                    mm(PS[bb][:, :], Wgo_v[:, g, dch * 128:(dch + 1) * 128], OG[:, g, :], g == 0, g == 3,
                       [kgo] + OGK, [f"ps{bb}"])
                tt(DVE, TM2[i2][:, :], PS[bb][:, :], SGB[i2][:, :], ALU.mult, [f"ps{bb}", f"sgb{i2}"], [f"tm2_{i2}"])
                tt(DVE if dch == 7 else POOL, MT[:, dch, :], TM1[i2][:, :], TM2[i2][:, :], ALU.add, [f"tm1_{i2}", f"tm2_{i2}"], [f"mt{dch}"])
            for _ in range(2 if half == 0 else 4):
                done_piece()
        act(WARM[:, 0:1], CST[:, C_EPS:C_EPS + 1], AF.Ln, ["cst"], ["warm"])
        for half in range(2):
            W, wk = next_piece(10 + half)
            Wv = W[:, :].rearrange("p (k n) -> p k n", n=512)
            for jb in range(4):
                b = nbank()
                for dch in range(8):
                    mm(PS[b][:, :], MT[:, dch, jb * 128:(jb + 1) * 128], Wv[:, dch, :], dch == 0, dch == 7,
                       [wk, f"mt{dch}"], [f"ps{b}"])
                tt(DVE, Hh[:, jb, half * 512:(half + 1) * 512], PS[b][:, :], Hh[:, jb, half * 512:(half + 1) * 512],
                   ALU.add, [f"ps{b}", f"h{hb}_{jb}"], [f"h{hb}_{jb}"])
            done_piece()

        P.alias(KEYS_MT, KEYS_HN)
        P.alias(KEYS_GLA, KEYS_HDN)
        P.alias(KEYS_G, KEYS_R32 + KEYS_OUT)
        norm_a(hb, per_jb=True)
        norm_b(C_G2P)
        if nxt is not None:
            load_x(*nxt)
        for q in range(8):
            W, wk = next_piece(12 + q)
            Wv = W[:, :].rearrange("p (k n) -> p k n", n=512)
            for fl in range(4):
                fch = q * 4 + fl
                i4 = fch % 4
                b = nbank()
                for kc in range(8):
                    mm(PS[b][:, :], Wv[:, kc, fl * 128:(fl + 1) * 128], UT[:, kc, :], kc == 0, kc == 7,
                       [wk, f"ut{kc}"], [f"ps{b}"])
                act(R32[i4][:, :], PS[b][:, :], AF.Relu, [f"ps{b}"], [f"r32_{i4}"])
                tt(POOL if fch % 2 else DVE, HDN[:, fch, :], R32[i4][:, :], R32[i4][:, :], ALU.mult,
                   [f"r32_{i4}"], [f"hdn{fch}"])
            done_piece()
        for half in range(2):
            accb = [nbank() for _ in range(4)]
            for g4 in range(4):
                W, wk = next_piece(20 + half * 4 + g4)
                Wv = W[:, :].rearrange("p (k n) -> p k n", n=512)
                for fl in range(8):
                    fch = g4 * 8 + fl
                    for jb in range(4):
                        mm(PS[accb[jb]][:, :], HDN[:, fch, jb * 128:(jb + 1) * 128], Wv[:, fl, :],
                           fch == 0, fch == 31, [wk, f"hdn{fch}"], [f"ps{accb[jb]}"])
                done_piece()
            if half == 1 and nxt is not None:
                norm_a(nxt[3])
                norm_b(C_G1P, banks=[b for b in range(NB) if b not in accb])
            for jb in range(4):
                tt(DVE, Hh[:, jb, half * 512:(half + 1) * 512], PS[accb[jb]][:, :],
                   Hh[:, jb, half * 512:(half + 1) * 512], ALU.add, [f"ps{accb[jb]}", f"h{hb}_{jb}"], [f"h{hb}_{jb}"])

        def stage_i_act():
            for jb in range(4):
                act(JNK[:, jb % 2, :], Hh[:, jb, :], AF.Square, [f"h{hb}_{jb}"], [f"r32_{jb % 2}", f"ss{jb}"],
                    accum_out=SS[:, jb:jb + 1])
            act(RS[:, :], SS[:, :], AF.Ln, [f"ss{i}" for i in range(4)] + ["cst"], [f"rs{i}" for i in range(4)],
                scale=1.0 / D, bias=CST[:, C_EPS:C_EPS + 1])
            act(RS[:, :], RS[:, :], AF.Exp, [f"rs{i}" for i in range(4)], [f"rs{i}" for i in range(4)], scale=-0.5)

        def stage_i_out():
            for jb in range(4):
                ob = jb % 2
                stt(DVE, OUTB[:, ob, :], Hh[:, jb, :], RS[:, jb:jb + 1], CST[:, C_G3:C_G3 + D], ALU.mult, ALU.mult,
                    [f"h{hb}_{jb}", f"rs{jb}", "cst"], [f"outb{ob}"])
                dma(out[s, j * T + jb * 128:j * T + (jb + 1) * 128, :], OUTB[:, ob, :], [f"outb{ob}"],
                    [f"out_{s}_{j}_{jb}"], f"d_out{ob}")

        if nxt is None:
            stage_i_act()
            stage_i_out()
        else:
            pending_i[0] = (stage_i_act, stage_i_out)

    pending_i = [None]
    tiles = [("meta", 0, 0)] + [("main", s, j) for s in range(n_seq) for j in range(n_tiles)]
    norm_a(0)
    norm_b(C_G1P)
    for i, (kind, s, j) in enumerate(tiles):
        hb = i % 2
        nxt = None
        if i + 1 < len(tiles):
            nk, ns, nj = tiles[i + 1]
            nxt = (nk, ns, nj, (i + 1) % 2)
        emit_body(kind, s, j, hb, nxt)

    resolve(P)
    final_dma = [(sems[k], v) for k, v in P.dma_count.items() if k.startswith("d_out")]
    with es:
        with nc.Block() as block:
            def emit_engine(eng_name):
                def body(e):
                    for op in P.by_eng[eng_name]:
                        for w in op.waits:
                            if w[0] == "dma":
                                e.wait_ge(sems[w[1]], w[2])
                            else:
                                d = P.ops[w[2]]
                                e.wait_ge(sems["e_" + d.eng], d.rank)
                        ins = op.fn(e)
                        if op.dma_sem is not None:
                            ins.then_inc(sems[op.dma_sem], 16)
                        elif op.target:
                            ins.then_inc(sems["e_" + eng_name], 1)
                    if eng_name == SP:
                        for sm, v in final_dma:
                            e.wait_ge(sm, v)
                return body

            block.tensor(emit_engine(PE))
            block.scalar(emit_engine(ACT))
            block.vector(emit_engine(DVE))
            block.gpsimd(emit_engine(POOL))
            block.sync(emit_engine(SP))
    return nc


def resolve(P):
    for op in P.ops:
        E = op.eng
        lst = P.by_eng[E]
        prev = lst[op.pos - 1] if op.pos > 0 else None
        clock = dict(prev.clock) if prev is not None else {}
        waits = []
        for d_idx in sorted(op.deps):
            kind = op.deps[d_idx]
            d = P.ops[d_idx]
            if d.dma_sem is not None:
                key = ("dma", d.dma_sem)
                if clock.get(key, 0) >= d.dma_val:
                    continue
                waits.append(("dma", d.dma_sem, d.dma_val))
                clock[key] = d.dma_val
                continue
            if d.eng == SP:
                continue
            if d.eng == E and E == PE:
                continue
            if clock.get(d.eng, -1) >= d.pos:
                continue
            waits.append(("eng", d.eng, d_idx))
            d.target = True
            for k, v in d.clock.items():
                if clock.get(k, -1 if not isinstance(k, tuple) else 0) < v:
                    clock[k] = v
            clock[d.eng] = max(clock.get(d.eng, -1), d.pos)
        op.waits = waits
        op.clock = clock
    for e in COMPUTE:
        r = 0
        for op in P.by_eng[e]:
            if op.target:
                r += 1
                op.rank = r


def _host_consts(norm_mix, norm_ffn, norm_final, pool_scale, gla_norm, b_gate):
    c = np.zeros((128, NC32), np.float32)
    s_idx = np.arange(128)[:, None]
    t_idx = np.arange(128)[None, :]
    same = (s_idx // 64) == (t_idx // 64)
    m1 = (same & (s_idx <= t_idx)).astype(np.float32)
    m2 = (same & (s_idx > t_idx)).astype(np.float32)
    c[:, C_MCAT:C_MCAT + 512] = np.concatenate([m1, m1, m2, m2], axis=1)
    cm = np.ones(512, np.float32)
    cm[::64] = 0.0
    c[:, C_CM:C_CM + 512] = cm[None, :]
    c[:, C_ONES:C_ONES + 128] = np.array([0x3F803F80], np.uint32).view(np.float32)[0]
    c[:, C_G3:C_G3 + D] = norm_final[None, :]
    c[:, C_G1P:C_G1P + 8] = norm_mix.reshape(8, 128).T
    c[:, C_G2P:C_G2P + 8] = norm_ffn.reshape(8, 128).T
    c[:, C_PSC:C_PSC + 4] = pool_scale.reshape(4, 128).T
    c[:, C_GNP:C_GNP + 4] = gla_norm.reshape(4, 128).T
    c[:, C_BG:C_BG + 2] = b_gate.reshape(2, 128).T
    c[:, C_EPS] = EPS
    cb = np.zeros((128, NCB), np.float32)
    cb[:, B_ID:B_ID + 128] = np.eye(128, dtype=np.float32)
    for g, w in enumerate((2, 4, 8, 16)):
        dlt = t_idx - s_idx
        cur = ((dlt >= 0) & (dlt <= w - 1)).astype(np.float32) / w - (dlt == 0).astype(np.float32)
        dlp = t_idx + 128 - s_idx
        prv = ((dlp >= 0) & (dlp <= w - 1)).astype(np.float32) / w
        cb[:, B_BC + g * 128:B_BC + (g + 1) * 128] = cur
        cb[:, B_BP + g * 128:B_BP + (g + 1) * 128] = prv
    return c, cb


def _lay(w, nk):
    return np.ascontiguousarray(w.reshape(nk, 128, w.shape[-1]).transpose(1, 0, 2))


_NC_CACHE = {}


def run(x, meta_tokens, norm_mix, w_in, w_pool_mix, pool_scale, w_pool_out, w_gate_up, b_gate,
        gla_norm, w_gla_out, w_out, norm_ffn, w_ff1, w_ff2, norm_final, n_cores=N_CORES):
    x = np.asarray(x, np.float32)
    B, L, _ = x.shape
    n_seq = B // n_cores
    n_tiles = L // T
    key = (n_seq, n_tiles)
    if key not in _NC_CACHE:
        _NC_CACHE[key] = build(n_seq, n_tiles)
    nc = _NC_CACHE[key]
    f = lambda a: np.asarray(a, np.float32)
    c32, cb = _host_consts(f(norm_mix)[0], f(norm_ffn)[0], f(norm_final), f(pool_scale)[0], f(gla_norm)[0],
                           f(b_gate)[0])
    xmeta = np.zeros((T, D), np.float32)
    xmeta[T - 16:] = f(meta_tokens)
    w_in_l = _lay(f(w_in)[0], 8)
    w_ff1_l = _lay(f(w_ff1)[0], 8)
    w_ff2_l = _lay(f(w_ff2)[0], 32)
    w_out_l = _lay(f(w_out)[0], 8)
    wall = np.empty((28, 128, 4096), np.float32)
    in_cols = {0: 512, 1: 0, 2: 1024, 3: 1536, 4: 2064, 5: 3088, 8: 2576, 9: 3600}
    for pc in range(28):
        if pc in in_cols:
            c0 = in_cols[pc]
            blk = w_in_l[:, :, c0:c0 + 512]
        elif pc == 6:
            blk = _lay(f(w_pool_out)[0], 4)
        elif pc == 7:
            blk = _lay(f(w_gla_out)[0], 4)
        elif pc in (10, 11):
            hh = pc - 10
            blk = w_out_l[:, :, hh * 512:(hh + 1) * 512]
        elif 12 <= pc < 20:
            q = pc - 12
            blk = w_ff1_l[:, :, q * 512:(q + 1) * 512]
        else:
            q = pc - 20
            hh, g4 = q // 4, q % 4
            blk = w_ff2_l[:, g4 * 8:(g4 + 1) * 8, hh * 512:(hh + 1) * 512]
        wall[pc] = blk.reshape(128, 4096)
    shared = {
        "xmeta": xmeta, "cst32": c32, "cstb": cb, "wall": wall,
        "w_lr_l": np.ascontiguousarray(w_in_l[:, :, 2048:2064]),
        "w_pm_l": np.ascontiguousarray(f(w_pool_mix)[0].transpose(1, 0, 2)),
        "w_gu": np.ascontiguousarray(f(w_gate_up)[0]),
    }
    in_maps = []
    for c in range(n_cores):
        m = dict(shared)
        m["x"] = np.ascontiguousarray(x[c * n_seq:(c + 1) * n_seq])
        in_maps.append(m)
    res = run_bass_kernel_spmd(nc, in_maps, core_ids=list(range(n_cores)))
    return np.concatenate([np.asarray(r["out"], np.float32) for r in res.results], axis=0)


def kernel(**inputs):
    return run(**inputs)
```
